# Optimizing a Trainium2 kernel written in Bass

```python
import math
import jax, jax.numpy as jnp
from jax import lax
import numpy as np

D_MODEL = 2048
BATCH = 4
SEQ = 4096
DEPTH = 4

D_MIX = D_MODEL
HEAD_DIM = 128
N_HEADS_A = 8
N_HEADS_B = 8
D_A = N_HEADS_A * HEAD_DIM
D_B = N_HEADS_B * HEAD_DIM
IDX_HEADS = 16
IDX_DIM = 64
TOPK_MAX = 256
Q_BLOCK = 128
RET_CHUNK = 128
N_BUCKETS = 32
MAX_EXACT = 16
MAX_DISTANCE = 128
ROPE_BASE = 10000.0
EPS = 1e-6
IN_SIZES = (D_A, HEAD_DIM, HEAD_DIM, D_A, IDX_HEADS * IDX_DIM, IDX_DIM, IDX_HEADS, D_B, D_B, D_B, D_B)
N_IN = 2 * D_A + 2 * HEAD_DIM + IDX_HEADS * IDX_DIM + IDX_DIM + IDX_HEADS + 4 * D_B

kernel_name = "hymba_dsa_retnet_hybrid"


def rmsnorm(x, g):
    xf = x.astype(jnp.float32)
    y = xf * lax.rsqrt(jnp.mean(xf * xf, axis=-1, keepdims=True) + EPS)
    return (y * g.astype(jnp.float32)).astype(x.dtype)


def head_groupnorm(x, g):
    xf = x.astype(jnp.float32)
    mu = jnp.mean(xf, axis=-1, keepdims=True)
    var = jnp.mean(jnp.square(xf - mu), axis=-1, keepdims=True)
    y = (xf - mu) * lax.rsqrt(var + EPS)
    B, S, H, D = x.shape
    return (y.reshape(B, S, H * D) * g.astype(jnp.float32)).astype(x.dtype)


def rotary(x, pos):
    half = x.shape[-1] // 2
    inv = ROPE_BASE ** (-jnp.arange(half, dtype=jnp.float32) / half)
    ang = pos.astype(jnp.float32)[..., None] * inv
    cos = jnp.cos(ang)[:, :, None, :]
    sin = jnp.sin(ang)[:, :, None, :]
    xf = x.astype(jnp.float32)
    x1, x2 = xf[..., :half], xf[..., half:]
    return jnp.concatenate([x1 * cos - x2 * sin, x1 * sin + x2 * cos], axis=-1).astype(x.dtype)


def t5_bucket(rel):
    rel = jnp.maximum(rel, 0)
    relf = jnp.maximum(rel, 1).astype(jnp.float32)
    large = MAX_EXACT + (jnp.log(relf / MAX_EXACT) / math.log(MAX_DISTANCE / MAX_EXACT)
                         * (N_BUCKETS - MAX_EXACT)).astype(jnp.int32)
    large = jnp.minimum(large, N_BUCKETS - 1)
    return jnp.where(rel < MAX_EXACT, rel, large)


def sparse_attention(q, k, v, q_idx, k_idx, w_idx, pos, rel_bias):
    B, S, H, Dh = q.shape
    n_blk = S // Q_BLOCK
    top_k = min(TOPK_MAX, S // 4)
    scale = HEAD_DIM ** -0.5
    key_pos = jnp.arange(S)

    def block(xs):
        qb, qib, wb, pb, start = xs
        t = start + jnp.arange(Q_BLOCK)
        dots = jnp.einsum('bqhd,bsd->bqhs', qib, k_idx).astype(jnp.float32)
        I = jnp.einsum('bqh,bqhs->bqs', wb.astype(jnp.float32), jax.nn.relu(dots))
        causal = key_pos[None, :] <= t[:, None]
        I = jnp.where(causal[None], I, -jnp.inf)
        _, sel = lax.top_k(I, top_k)
        valid = sel <= t[None, :, None]
        ks = jax.vmap(lambda kb, ib: kb[ib])(k, sel)
        vs = jax.vmap(lambda vb, ib: vb[ib])(v, sel)
        ps = jax.vmap(lambda pb_, ib: pb_[ib])(pos, sel)
        bias = rel_bias[t5_bucket(pb[:, :, None] - ps)]
        bias = jnp.moveaxis(bias, -1, 2).astype(jnp.float32)
        logits = jnp.einsum('bqhd,bqkd->bqhk', qb, ks).astype(jnp.float32) * scale + bias
        logits = jnp.where(valid[:, :, None, :], logits, -jnp.inf)
        p = jax.nn.softmax(logits, axis=-1).astype(v.dtype)
        return jnp.einsum('bqhk,bqkd->bqhd', p, vs)

    def to_blocks(a):
        return a.reshape((B, n_blk, Q_BLOCK) + a.shape[2:]).swapaxes(0, 1)

    xs = (to_blocks(q), to_blocks(q_idx), to_blocks(w_idx), to_blocks(pos),
          jnp.arange(n_blk) * Q_BLOCK)
    out = lax.map(block, xs)
    return out.swapaxes(0, 1).reshape(B, S, H, Dh)


def retention(q, k, v):
    B, S, H, D = q.shape
    C = RET_CHUNK
    nC = S // C
    log_g = jnp.log1p(-jnp.exp2(-5.0 - jnp.arange(H, dtype=jnp.float32)))
    i = jnp.arange(C, dtype=jnp.float32)
    diff = i[:, None] - i[None, :]
    dmask = jnp.where(diff >= 0, jnp.exp(log_g[:, None, None] * jnp.maximum(diff, 0.0)), 0.0)
    qc = q.astype(jnp.float32).reshape(B, nC, C, H, D)
    kc = k.astype(jnp.float32).reshape(B, nC, C, H, D)
    vc = v.astype(jnp.float32).reshape(B, nC, C, H, D)
    s_in = jnp.einsum('bnihd,bnjhd->bnhij', qc, kc) * dmask
    intra = jnp.einsum('bnhij,bnjhd->bnihd', s_in, vc)
    zeta = jnp.exp(log_g[None, :] * (C - 1.0 - i)[:, None])
    u = jnp.einsum('bnjhd,bnjhe->bnhde', kc * zeta[None, None, :, :, None], vc)
    g_chunk = jnp.exp(log_g * C)[None, :, None, None]

    def step(R, u_n):
        return g_chunk * R + u_n, R

    _, R_prev = lax.scan(step, jnp.zeros((B, H, D, D), jnp.float32), u.swapaxes(0, 1))
    R_prev = R_prev.swapaxes(0, 1)
    xi = jnp.exp(log_g[None, :] * (i + 1.0)[:, None])
    cross = jnp.einsum('bnihd,bnhde->bnihe', qc, R_prev) * xi[None, None, :, :, None]
    return (intra + cross).reshape(B, S, H, D)


def setup_inputs(seed: int = 0) -> dict:
    key = jax.random.key(seed)
    ks = jax.random.split(key, 12)
    f32 = jnp.float32
    x = jax.random.normal(ks[0], (BATCH, SEQ, D_MODEL), f32)
    c = jax.random.normal(ks[1], (BATCH, D_MODEL), f32)
    positions = jnp.broadcast_to(jnp.arange(SEQ, dtype=jnp.int32)[None, :], (BATCH, SEQ))
    rel_bias = 0.5 * jax.random.normal(ks[2], (N_BUCKETS, N_HEADS_A), f32)
    norm_gain = 1.0 + 0.05 * jax.random.normal(ks[3], (DEPTH, D_MODEL), f32)
    w_mod = 0.5 * D_MODEL ** -0.5 * jax.random.normal(ks[4], (DEPTH, D_MODEL, 3 * D_MODEL), f32)
    b_mod = 0.02 * jax.random.normal(ks[5], (DEPTH, 3 * D_MODEL), f32)
    w_in = D_MODEL ** -0.5 * jax.random.normal(ks[6], (DEPTH, D_MODEL, N_IN), f32)
    q_norm_gain = 1.0 + 0.05 * jax.random.normal(ks[7], (DEPTH, HEAD_DIM), f32)
    k_norm_gain = 1.0 + 0.05 * jax.random.normal(ks[8], (DEPTH, HEAD_DIM), f32)
    ret_norm_gain = 1.0 + 0.05 * jax.random.normal(ks[9], (DEPTH, D_B), f32)
    w_out = D_MIX ** -0.5 * jax.random.normal(ks[10], (DEPTH, D_MIX, D_MODEL), f32)
    return {"x": x, "c": c, "positions": positions, "rel_bias": rel_bias,
            "norm_gain": norm_gain, "w_mod": w_mod, "b_mod": b_mod, "w_in": w_in,
            "q_norm_gain": q_norm_gain, "k_norm_gain": k_norm_gain,
            "ret_norm_gain": ret_norm_gain, "w_out": w_out}


def reference(x, c, positions, rel_bias, norm_gain, w_mod, b_mod, w_in,
              q_norm_gain, k_norm_gain, ret_norm_gain, w_out):
    B, S, _ = x.shape
    split_points = [int(p) for p in np.cumsum(IN_SIZES)[:-1]]
    c_act = jax.nn.silu(c)
    for l in range(DEPTH):
        mod = c_act @ w_mod[l] + b_mod[l]
        shift, scale, gate = jnp.split(mod, 3, axis=-1)
        h = rmsnorm(x, norm_gain[l]) * (1.0 + scale[:, None, :]) + shift[:, None, :]
        proj = h @ w_in[l]
        q_a, k_a, v_a, g_a, q_i, k_i, w_i, q_b, k_b, v_b, g_b = jnp.split(proj, split_points, axis=-1)
        q_a = rmsnorm(q_a.reshape(B, S, N_HEADS_A, HEAD_DIM), q_norm_gain[l])
        k_a = rmsnorm(k_a, k_norm_gain[l])
        att = sparse_attention(q_a, k_a, v_a, q_i.reshape(B, S, IDX_HEADS, IDX_DIM), k_i, w_i,
                               positions, rel_bias)
        y_a = att.reshape(B, S, D_A) * jax.nn.silu(g_a)
        q_b = rotary(q_b.reshape(B, S, N_HEADS_B, HEAD_DIM), positions)
        k_b = rotary(k_b.reshape(B, S, N_HEADS_B, HEAD_DIM), positions) * (HEAD_DIM ** -0.5)
        ret = retention(q_b, k_b, v_b.reshape(B, S, N_HEADS_B, HEAD_DIM))
        y_b = head_groupnorm(ret, ret_norm_gain[l]).astype(x.dtype) * jax.nn.silu(g_b)
        y = jnp.concatenate([y_a, y_b], axis=-1) @ w_out[l]
        x = x + gate[:, None, :] * y
    return x
```

```python
import math
from contextlib import ExitStack
import numpy as np
import concourse.bass as bass
import concourse.mybir as mybir
from concourse.bass_utils import run_bass_kernel_spmd

F32 = mybir.dt.float32
BF16 = mybir.dt.bfloat16
I32 = mybir.dt.int32
AF = mybir.ActivationFunctionType
ALU = mybir.AluOpType
AX = mybir.AxisListType

D = 2048
HD = 128
NH = 8
NIH = 16
DEPTH = 4
SEQ = 4096
EPS = 1e-6
NEG = -1.0e30
N_IN = 7504
GROUPS = ["qa", "ga", "qi", "qb", "kb", "vb", "gb", "misc"]
ORIG = {"qa": (0, 1024), "ka": (1024, 1152), "va": (1152, 1280), "ga": (1280, 2304),
        "qi": (2304, 3328), "ki": (3328, 3392), "wi": (3392, 3408), "qb": (3408, 4432),
        "kb": (4432, 5456), "vb": (5456, 6480), "gb": (6480, 7504)}
PERM = np.concatenate([np.arange(*ORIG[k]) for k in
                       ["qa", "ga", "qi", "qb", "kb", "vb", "gb", "ka", "va", "ki", "wi"]])
NBIS = 13
import os as _os
BIS_SPLIT = int(_os.environ.get('BIS_SPLIT', '0'))


class Buf:
    __slots__ = ("w", "r", "x")

    def __init__(self, x=False):
        self.w = None
        self.r = {}
        self.x = x


class T:
    def __init__(self, ap, bufs):
        self.ap = ap
        self.bufs = bufs if isinstance(bufs, list) else [bufs]

    def __getitem__(self, k):
        return T(self.ap[k], self.bufs)

    def v(self, f):
        return T(f(self.ap), self.bufs)

    def h3(self, h=NH):
        return T(self.ap.rearrange("p (h d) -> p h d", h=h), self.bufs)


def bch(t, n):
    return T(t.ap.unsqueeze(2).broadcast_to([t.ap.shape[0], t.ap.shape[1], n]), t.bufs)


def bcm(t, h):
    return T(t.ap.unsqueeze(1).broadcast_to([t.ap.shape[0], h, t.ap.shape[1]]), t.bufs)


class Rot:
    def __init__(self, items):
        self.items = items
        self.i = 0

    def next(self):
        t = self.items[self.i]
        self.i = (self.i + 1) % len(self.items)
        return t


NDS = 24


class Sched:
    def __init__(self, nc, stack):
        self.nc = nc
        self.names = ["pe", "act", "dve", "pool", "sp"]
        self.ops = {k: [] for k in self.names}
        self.esem = {k: stack.enter_context(nc.semaphore("es_" + k)) for k in ("pe", "act", "dve", "pool")}
        self.cnt = {k: 0 for k in self.esem}
        self.seen = {k: {} for k in self.names}
        self.dsems = [stack.enter_context(nc.semaphore("ds%d" % i)) for i in range(NDS)]
        self.dcnt = [0] * NDS
        self.dnext = 0

    def _need(self, e, need, ev):
        if ev is None:
            return
        k, v = ev
        if e == "pe" and k == "pe":
            return
        if self.seen[e].get(k, 0) >= v:
            return
        if need.get(k, 0) < v:
            need[k] = v

    def _deps(self, e, reads, writes):
        need = {}
        for t in reads:
            for b in t.bufs:
                self._need(e, need, b.w)
                if b.x:
                    for k, v in b.r.items():
                        if k != e:
                            self._need(e, need, (k, v))
        for t in writes:
            for b in t.bufs:
                self._need(e, need, b.w)
                for k, v in b.r.items():
                    self._need(e, need, (k, v))
        for k, v in need.items():
            self.seen[e][k] = v
        return list(need.items())

    def op(self, e, fn, reads, writes):
        need = self._deps(e, reads, writes)
        self.cnt[e] += 1
        v = self.cnt[e]
        self.ops[e].append((need, fn, e))
        for t in reads:
            for b in t.bufs:
                b.r[e] = v
        for t in writes:
            for b in t.bufs:
                b.w = (e, v)
                b.r = {}

    def dma(self, q, out, in_):
        i = self.dnext
        self.dnext = (i + 1) % NDS
        key = ("d", i)
        need = self._deps(q, [in_], [out])
        if self.dcnt[i] > 0 and self.seen[q].get(key, 0) < self.dcnt[i]:
            need.append((key, self.dcnt[i]))
            self.seen[q][key] = self.dcnt[i]
        self.dcnt[i] += 16
        v = self.dcnt[i]
        oa, ia = out.ap, in_.ap
        self.ops[q].append((need, lambda eng: eng.dma_start(out=oa, in_=ia), key))
        for b in in_.bufs:
            b.r[key] = v
        for b in out.bufs:
            b.w = (key, v)
            b.r = {}

    def semh(self, k):
        return self.dsems[k[1]] if isinstance(k, tuple) else self.esem[k]

    def emit(self):
        fin = [(("d", i), c) for i, c in enumerate(self.dcnt) if c > 0]
        fin += [(k, c) for k, c in self.cnt.items() if c > 0]
        nc = self.nc

        def mk(name):
            def body(eng):
                for need, fn, inc in self.ops[name]:
                    for k, v in need:
                        eng.wait_ge(self.semh(k), v)
                    ins = fn(eng)
                    if isinstance(inc, tuple):
                        ins.then_inc(self.dsems[inc[1]], 16)
                    else:
                        ins.then_inc(self.esem[inc], 1)
                if name == "sp":
                    for k, v in fin:
                        eng.wait_ge(self.semh(k), v)
            return body

        with nc.Block() as block:
            block.tensor(mk("pe"))
            block.scalar(mk("act"))
            block.vector(mk("dve"))
            block.gpsimd(mk("pool"))
            block.sync(mk("sp"))

    def mm(self, out, lhsT, rhs, start, stop, sgc=False):
        o, l, r = out.ap, lhsT.ap, rhs.ap
        self.op("pe", lambda e: e.matmul(o, l, r, start=start, stop=stop, skip_group_check=sgc),
                [lhsT, rhs], [out])

    def tr(self, out, in_, ident):
        o, i, d = out.ap, in_.ap, ident.ap
        self.op("pe", lambda e: e.transpose(o, i, d), [in_, ident], [out])

    def act(self, out, in_, func, scale=None, bias=None, accum=None):
        o, i = out.ap, in_.ap
        kw = {}
        rd = [in_]
        wr = [out]
        if scale is not None:
            if isinstance(scale, T):
                kw["scale"] = scale.ap
                rd.append(scale)
            else:
                kw["scale"] = scale
        if bias is not None:
            if isinstance(bias, T):
                kw["bias"] = bias.ap
                rd.append(bias)
            else:
                kw["bias"] = bias
        if accum is not None:
            kw["accum_out"] = accum.ap
            wr.append(accum)
        self.op("act", lambda e: e.activation(o, i, func, **kw), rd, wr)

    def ts(self, eng, out, in0, s1, s2, op0, op1=None, accum=None):
        o, i = out.ap, in0.ap
        rd = [in0]
        wr = [out]
        a1 = s1
        a2 = s2
        if isinstance(s1, T):
            a1 = s1.ap
            rd.append(s1)
        if isinstance(s2, T):
            a2 = s2.ap
            rd.append(s2)
        kw = {}
        if op1 is not None:
            kw["op1"] = op1
        if accum is not None:
            kw["accum_out"] = accum.ap
            wr.append(accum)
        self.op(eng, lambda e: e.tensor_scalar(o, i, a1, a2, op0, **kw), rd, wr)

    def tt(self, eng, out, in0, in1, op):
        o, a, b = out.ap, in0.ap, in1.ap
        self.op(eng, lambda e: e.tensor_tensor(o, a, b, op), [in0, in1], [out])

    def stt(self, out, in0, scalar, in1, op0, op1):
        o, a, b = out.ap, in0.ap, in1.ap
        rd = [in0, in1]
        s = scalar
        if isinstance(scalar, T):
            s = scalar.ap
            rd.append(scalar)
        self.op("dve", lambda e: e.scalar_tensor_tensor(o, a, s, b, op0, op1), rd, [out])

    def red(self, out, in_, op, axis=AX.X):
        o, i = out.ap, in_.ap
        self.op("dve", lambda e: e.tensor_reduce(o, i, axis, op), [in_], [out])

    def cp(self, eng, out, in_):
        o, i = out.ap, in_.ap
        if eng == "act":
            self.op("act", lambda e: e.copy(o, i), [in_], [out])
        else:
            self.op(eng, lambda e: e.tensor_copy(o, i), [in_], [out])

    def recip(self, out, in_):
        o, i = out.ap, in_.ap
        self.op("dve", lambda e: e.reciprocal(o, i), [in_], [out])


def _t5_bucket(rel):
    rel = np.maximum(rel, 0)
    relf = np.maximum(rel, 1).astype(np.float32)
    large = 16 + (np.log(relf / np.float32(16)) / np.float32(math.log(128 / 16)) * np.float32(16)).astype(np.int32)
    large = np.minimum(large, 31)
    return np.where(rel < 16, rel, large)


def build(NL=DEPTH, NTT=32, TOPK=256, debug=False, upto=None):
    S = NTT * 128
    nc = bass.Bass("TRN2", target_bir_lowering=False)
    st = ExitStack()
    kout = "ExternalOutput" if debug else "Internal"

    def din(name, shape, dt=F32):
        return nc.dram_tensor(name, shape, dt, kind="ExternalInput").ap()

    def dsc(name, shape, dt=F32, kind=None):
        return nc.dram_tensor(name, shape, dt, kind=kind or kout).ap()

    x_d = din("x", [S, D])
    c_d = din("c", [128, 16])
    pos_d = din("positions", [128, NTT], I32)
    ng_d = din("norm_gain", [NL, D])
    wmod_d = din("w_mod", [NL, D, 3 * D])
    bmod_d = din("b_mod", [NL, 3 * D])
    win_d = din("w_in", [NL, D, N_IN])
    qg_d = din("q_norm_gain", [NL, 128, 1])
    kg_d = din("k_norm_gain", [NL, 128, 1])
    rg_d = din("ret_norm_gain", [NL, 1024])
    wout_d = din("w_out", [NL, D, D])
    ident_d = din("ident", [128, 128])
    caus_d = din("causT", [128, 128])
    negtri_d = din("negtri", [128, 128])
    invf_d = din("invf", [64])
    decq_d = din("decq", [128, 8])
    deck_d = din("deck", [128, 8])
    gc_d = din("gc", [1024])
    pow2_d = din("pow2", [128, 32])
    bt0_d = din("bt0", [128, 1024])
    bt1_d = din("bt1", [128, 1024])
    b31_d = din("b31", [128, 1024])
    out_d = nc.dram_tensor("out", [S, D], F32, kind="ExternalOutput").ap()

    modraw_d = dsc("modraw", [NL, 3 * D])
    cs_d = dsc("cs", [NTT, 128, 128])
    hT_d = dsc("hT", [NTT, 128, 2048], BF16)
    QT_d = dsc("QT", [NTT, 128, 1024], BF16)
    QiT_d = dsc("QiT", [NTT, 128, 1024], BF16)
    GA_d = dsc("GA", [NTT, 128, 1024])
    QB_d = dsc("QB", [NTT, 128, 1024], BF16)
    KB_d = dsc("KB", [NTT, 128, 1024], BF16)
    VB_d = dsc("VB", [NTT, 128, 1024], BF16)
    GB_d = dsc("GB", [NTT, 128, 1024])
    yT_d = dsc("yT", [NTT, 128, 2048], BF16)
    xs_d = dsc("xs", [S, D], kind="Internal")
    if debug:
        dbgI_d = dsc("dbgI", [NTT, 128, S])
        dbgT_d = dsc("dbgT", [NTT, 128, 8])
        dbgM_d = dsc("dbgM", [NTT, 128, S], BF16)
        dbgO_d = dsc("dbgO", [NTT, 128, 1032])

    def sb(name, shape, dt=F32):
        return st.enter_context(nc.sbuf_tensor("s_" + name, shape, dt))

    ps_t = st.enter_context(nc.psum_tensor("ps", [128, 4096], F32))
    S_ = Sched(nc, st)

    def finish():
        print("sbuf bytes remaining", nc.sbuf_bytes_remaining)
        S_.emit()
        st.close()
        return nc
    psb = [Buf(True) for _ in range(8)]

    def PS(b, n=1):
        return T(ps_t[:, b * 512:(b + n) * 512], psb[b:b + n])

    def PSbf(b):
        return T(ps_t[:, b * 512:(b + 1) * 512].bitcast(BF16), [psb[b]])

    def tile(name, shape, dt=F32):
        return T(sb(name, shape, dt)[:], Buf())

    def dtile(ap):
        return T(ap, Buf())

    arena = sb("arena8", [128, 5, 2048], F32)
    F8K = [T(arena[:, i, :], Buf()) for i in range(5)]
    Iacc = T(arena[:, 0:2, :].rearrange("p a b -> p (a b)"), F8K[0].bufs + F8K[1].bufs)
    maskb = T(arena[:, 2, :].bitcast(BF16), F8K[2].bufs)
    maskT = T(arena[:, 3, :].bitcast(BF16), F8K[3].bufs)
    f8k = Rot(F8K[2:5])
    wsl_t = sb("wslab", [128, 3, 16, 512], BF16)
    WSL = Rot([T(wsl_t[:, i, :, :], Buf()) for i in range(3)])
    b4k_t = sb("b4k", [128, 4, 2048], BF16)
    B4K = Rot([T(b4k_t[:, i, :], Buf()) for i in range(4)])
    IaccB = T(b4k_t[:].rearrange("p a b -> p (a b)").bitcast(F32), [b for t_ in B4K.items for b in t_.bufs])
    IA = [Iacc, IaccB]
    f4k_t = sb("f4k", [128, 4, 1024], F32)
    F4K = Rot([T(f4k_t[:, i, :], Buf()) for i in range(4)])
    b2k_t = sb("b2k", [128, 8, 1024], BF16)
    B2K = Rot([T(b2k_t[:, i, :], Buf()) for i in range(8)])
    qp_t = sb("qp", [128, 4, 1024], BF16)
    QP = Rot([T(qp_t[:, i, :], Buf()) for i in range(4)])
    st_t = sb("stp", [128, 2, 32], F32)
    ST = Rot([T(st_t[:, i, :], Buf()) for i in range(2)])
    mid_t = sb("mid", [128, 4, 1], F32)
    MID = Rot([T(mid_t[:, i, :], Buf()) for i in range(4)])
    pow2 = tile("pow2", [128, 32])
    dgw_t = sb("dgw", [128, 1, 2048], BF16)
    DGW = Rot([T(dgw_t[:, i, :], Buf()) for i in range(1)])
    lh_t = sb("lohi", [128, 2, 4], F32)
    LH = Rot([T(lh_t[:, i, :], Buf()) for i in range(2)])
    f2k_t = sb("f2k", [128, 3, 512], F32)
    F2K = Rot([T(f2k_t[:, i, :], Buf()) for i in range(3)])
    R16 = Rot([T(f2k_t[:, i, :].bitcast(BF16)[:, 0:512], F2K.items[i].bufs) for i in range(3)])
    sm_t = sb("small", [128, 48, 8], F32)
    SM = Rot([T(sm_t[:, i, :], Buf()) for i in range(48)])
    cs_t = sb("cst", [128, 3, 128], F32)
    CS = Rot([T(cs_t[:, i, :], Buf()) for i in range(3)])

    kt_t = sb("KT", [128, S], BF16)
    kbufs = [Buf() for _ in range(NTT)]
    v_t = sb("V", [128, NTT, 129], BF16)
    vbufs = [Buf() for _ in range(NTT)]
    kit_t = sb("KiT", [128, S], BF16)
    kibufs = [Buf() for _ in range(NTT)]
    wi_t = sb("WI", [128, NTT, 16], F32)
    wibufs = [Buf() for _ in range(NTT)]
    R = tile("R", [128, 1024])
    Rbf = tile("Rbf", [128, 1024], BF16)

    ident = tile("ident", [128, 128], BF16)
    causT = tile("causTb", [128, 128], BF16)
    negtri = tile("negtri", [128, 128])
    EB0 = tile("EB0", [128, 1024], BF16)
    EB1 = tile("EB1", [128, 1024], BF16)
    GCt = tile("GCt", [128, 1024])
    RG = tile("RGt", [128, 1024])
    decq = tile("decq", [128, 8])
    deck = tile("deck", [128, 8])
    half = tile("half", [128, 1])
    ones = tile("ones", [128, 2], BF16)
    gkq = tile("gkq", [128, 1])
    gk2 = tile("gk2", [128, 1])
    cact = tile("cact", [128, 16])
    cin = tile("cin", [128, 16])
    modsb = tile("modsb", [1, 512])
    bmsb = tile("bmsb", [1, 512])

    def dbufs(ap3):
        return [T(ap3[i], Buf()) for i in range(ap3.shape[0])]

    hT_D = dbufs(hT_d)
    QT_D = dbufs(QT_d)
    QiT_D = dbufs(QiT_d)
    GA_D = dbufs(GA_d)
    QB_D = dbufs(QB_d)
    KB_D = dbufs(KB_d)
    VB_D = dbufs(VB_d)
    GB_D = dbufs(GB_d)
    cs_D = dbufs(cs_d)
    yTa_D = [T(yT_d[i][:, 0:1024], Buf()) for i in range(NTT)]
    yTb_D = [T(yT_d[i][:, 1024:2048], Buf()) for i in range(NTT)]
    xs_D = [T(xs_d[i * 128:(i + 1) * 128, :], Buf()) for i in range(NTT)]
    xin_D = [T(x_d[i * 128:(i + 1) * 128, :], Buf()) for i in range(NTT)]
    out_D = [T(out_d[i * 128:(i + 1) * 128, :], Buf()) for i in range(NTT)]
    modraw_B = Buf()

    def cst(ap):
        return T(ap, [])

    S_.dma("pool", ident, cst(ident_d))
    S_.dma("pool", causT, cst(caus_d))
    S_.dma("sp", negtri, cst(negtri_d))
    S_.dma("sp", decq, cst(decq_d))
    S_.dma("sp", deck, cst(deck_d))
    S_.dma("sp", GCt, cst(gc_d.partition_broadcast(128)))
    S_.dma("sp", cin, cst(c_d))
    S_.dma("sp", pow2, cst(pow2_d))
    S_.op("dve", lambda e: e.memset(half.ap, 0.5), [], [half])
    S_.op("dve", lambda e: e.memset(ones.ap, 1.0), [], [ones])
    S_.op("dve", lambda e: e.memset(v_t[:, :, 128:129], 1.0), [], [T(v_t[:, :, 128:129], vbufs)])
    for (bd, EB, diag) in ((bt0_d, EB0, True), (bt1_d, EB1, False)):
        a = F4K.next()
        b = F4K.next()
        S_.dma("sp", a, cst(bd))
        S_.dma("sp", b, cst(b31_d))
        S_.tt("dve", a, a, b, ALU.subtract)
        if diag:
            S_.act(b, a, AF.Exp)
            S_.tt("dve", EB.h3(), b.h3(), bcm(causT, NH), ALU.mult)
        else:
            S_.act(EB, a, AF.Exp)
    if upto == 'p0a':
        return finish()
    posi = T(sb("posi", [128, NTT], I32)[:], Buf())
    posf = tile("posf", [128, NTT])
    invf = tile("invf", [128, 64])
    S_.dma("sp", posi, cst(pos_d))
    S_.dma("sp", invf, cst(invf_d.partition_broadcast(128)))
    S_.cp("dve", posf, posi)
    TWO_PI = 2.0 * math.pi
    C1 = 6.28125
    C2 = TWO_PI - C1
    for t0 in range(0, NTT, 8):
        nt = min(8, NTT - t0)
        fk = F4K.items
        ang = fk[0][:, 0:nt * 64]
        nn = fk[1][:, 0:nt * 64]
        ni = T(fk[2].ap.bitcast(I32)[:, 0:nt * 64], fk[2].bufs)
        cst_t = fk[3]
        m = fk[2][:, 0:nt * 64]
        a3 = ang.v(lambda a: a.rearrange("p (t j) -> p t j", j=64))
        S_.tt("dve", a3, bch(posf[:, t0:t0 + nt], 64), bcm(invf, nt), ALU.mult)
        S_.ts("dve", nn, ang, 1.0 / TWO_PI, None, ALU.mult)
        S_.cp("dve", ni, nn)
        S_.cp("dve", nn, ni)
        S_.stt(ang, nn, -C1, ang, ALU.mult, ALU.add)
        S_.stt(ang, nn, -C2, ang, ALU.mult, ALU.add)
        for which, shift in ((1, 0.0), (0, 0.5 * math.pi)):
            r = nn
            S_.ts("dve", r, ang, shift, None, ALU.add)
            S_.ts("dve", m, r, math.pi, -TWO_PI, ALU.is_gt, ALU.mult)
            S_.tt("dve", r, r, m, ALU.add)
            S_.ts("dve", m, r, -math.pi, TWO_PI, ALU.is_lt, ALU.mult)
            S_.tt("dve", r, r, m, ALU.add)
            S_.ts("dve", r, r, math.pi, -math.pi, ALU.min, ALU.max)
            o = T(cst_t.ap[:, 0:nt * 128].rearrange("p (t c) -> p t c", c=128)[:, :, which * 64:(which + 1) * 64],
                  cst_t.bufs)
            S_.act(o, r.v(lambda a: a.rearrange("p (t j) -> p t j", j=64)), AF.Sin)
        for i in range(nt):
            S_.dma("pool", cs_D[t0 + i], cst_t[:, i * 128:(i + 1) * 128])
    if upto == 'p0b':
        return finish()
    S_.act(cact, cin, AF.Silu)
    for l in range(NL):
        for n in range(12):
            pb = PS(n % 4)
            for q4 in range(4):
                wst = f8k.next()
                w3 = wst.v(lambda a: a.rearrange("p (k n) -> p k n", k=4))
                S_.dma("sp" if q4 % 2 == 0 else "act", w3,
                       cst(wmod_d[l, q4 * 512:(q4 + 1) * 512, n * 512:(n + 1) * 512].rearrange("(k p) n -> p k n", p=128)))
                for k in range(4):
                    kc = q4 * 4 + k
                    S_.mm(pb[0:1, :], cact[:, kc:kc + 1], w3[:, k, :], start=(kc == 0), stop=(kc == 15))
            S_.dma("sp", bmsb, cst(bmod_d[l:l + 1, n * 512:(n + 1) * 512]))
            S_.tt("dve", modsb, pb[0:1, :], bmsb, ALU.add)
            S_.dma("pool", T(modraw_d[l:l + 1, n * 512:(n + 1) * 512], [modraw_B]), modsb)

    if upto == 'p0':
        return finish()
    def load_w_half(wd, l, c0, ncol):
        slab = WSL.next()
        for hh in range(2):
            S_.dma("pool", slab.v(lambda a: a[:, hh * 8:(hh + 1) * 8, 0:ncol]),
                   cst(wd[l, hh * 1024:(hh + 1) * 1024, c0:c0 + ncol].rearrange("(k p) n -> p k n", p=128)))
        return slab

    def wgroups(wd, l, specs):
        pre = None
        for gi, (c0, ncols) in enumerate(specs):
            nh = (ncols + 511) // 512
            slabs = []
            for i in range(nh):
                if i == 0 and pre is not None:
                    slabs.append(pre)
                else:
                    slabs.append(load_w_half(wd, l, c0 + i * 512, min(512, ncols - i * 512)))
            pre = None
            if gi + 1 < len(specs):
                n0, nn = specs[gi + 1]
                pre = load_w_half(wd, l, n0, min(512, nn))
            yield slabs

    def rstd_from(ss, n, inv_n):
        a = SM.next()[:, 0:n]
        S_.ts("dve", a, ss, inv_n, EPS, ALU.mult, ALU.add)
        b = SM.next()[:, 0:n]
        S_.act(b, a, AF.Sqrt)
        c = SM.next()[:, 0:n]
        S_.recip(c, b)
        return c

    for l in range(NL):
        xsrc = xin_D if l == 0 else xs_D
        xdst = out_D if l == NL - 1 else xs_D
        A_bc, sh_bc = F8K[0], F8K[1]
        tmpg = f8k.next()
        S_.dma("sp", A_bc, T(modraw_d[l, D:2 * D].partition_broadcast(128), [modraw_B]))
        S_.dma("sp", tmpg, cst(ng_d[l].partition_broadcast(128)))
        S_.stt(A_bc, A_bc, 1.0, tmpg, ALU.add, ALU.mult)
        S_.dma("sp", sh_bc, T(modraw_d[l, 0:D].partition_broadcast(128), [modraw_B]))
        S_.dma("sp", gkq, cst(qg_d[l]))
        S_.dma("sp", gk2, cst(kg_d[l]))
        S_.stt(gkq, gkq, HD ** -0.5, gk2, ALU.mult, ALU.mult)
        S_.dma("sp", RG, cst(rg_d[l].partition_broadcast(128)))

        gspecs = []
        c0 = 0
        for g in GROUPS:
            ncols = 1024 if g != "misc" else 336
            gspecs.append((c0, ncols))
            c0 += ncols
        wgen = wgroups(win_d, l, gspecs)
        slabs_first = next(wgen)
        for t in range(NTT):
            xt = f8k.next()
            S_.dma("sp", xt, xsrc[t])
            junk = B4K.next()
            ss = SM.next()[:, 0:1]
            S_.act(junk, xt, AF.Square, accum=ss)
            rs = rstd_from(ss, 1, 1.0 / D)
            tmp = f8k.next()
            S_.stt(tmp, xt, rs, A_bc, ALU.mult, ALU.mult)
            hb = B4K.next()
            S_.tt("dve", hb, tmp, sh_bc, ALU.add)
            for hh in range(2):
                pT = PSbf(6 + hh)
                for k in range(8):
                    kc = hh * 8 + k
                    S_.tr(pT[:, k * 128:(k + 1) * 128], hb[:, kc * 128:(kc + 1) * 128], ident)
            hT = B4K.next()
            S_.cp("act", hT[:, 0:1024], PSbf(6))
            S_.cp("dve", hT[:, 1024:2048], PSbf(7))
            S_.dma("pool", hT_D[t], hT)

        if upto == 'A1':
            return finish()
        c0 = 0
        a2i = 0
        for gidx, g in enumerate(GROUPS):
            ncols = 1024 if g != "misc" else 336
            slabs = slabs_first if gidx == 0 else next(wgen)
            deferred = None
            if upto == 'A2w':
                c0 += ncols
                continue
            for t in range(NTT):
                hT = B4K.next()
                S_.dma("sp", hT, hT_D[t])
                pb0 = (0, 2, 6)[a2i % 3]
                a2i += 1
                pp = PS(pb0, 2)
                for i, slab in enumerate(slabs):
                    nc_i = min(512, ncols - i * 512)
                    for kc in range(16):
                        S_.mm(PS(pb0 + i)[:, 0:nc_i], hT[:, kc * 128:(kc + 1) * 128],
                              slab.v(lambda a: a[:, kc, 0:nc_i]), start=(kc == 0), stop=(kc == 15))
                if upto == 'A2m' or (upto is not None and upto.startswith('A2g') and g not in upto[4:].split(',')):
                    continue
                if g == "qa":
                    sq = F4K.next()
                    S_.act(sq, pp, AF.Square)
                    ss = SM.next()
                    S_.red(ss, sq.h3(), ALU.add)
                    rs = rstd_from(ss, 8, 1.0 / HD)
                    qn = B2K.next()
                    S_.tt("dve", qn.h3(), pp.h3(), bch(rs, HD), ALU.mult)
                    def post(qn=qn, t=t):
                        pT = PSbf(4)
                        for h in range(8):
                            S_.tr(pT[:, h * 128:(h + 1) * 128], qn[:, h * 128:(h + 1) * 128], ident)
                        o = B2K.next()
                        S_.cp("act", o, pT)
                        S_.dma("pool", QT_D[t], o)
                    if deferred is not None:
                        deferred()
                    deferred = post
                elif g in ("ga", "gb"):
                    o = F4K.next()
                    S_.act(o, pp, AF.Silu)
                    S_.dma("pool", (GA_D if g == "ga" else GB_D)[t], o)
                elif g == "qi":
                    qn = B2K.next()
                    S_.cp("act", qn, pp)
                    def post(qn=qn, t=t):
                        pT = PSbf(5)
                        for h in range(8):
                            S_.tr(pT[:, h * 128:(h + 1) * 128], qn[:, h * 128:(h + 1) * 128], ident)
                        o = B2K.next()
                        S_.cp("dve", o, pT)
                        S_.dma("pool", QiT_D[t], o)
                    if deferred is not None:
                        deferred()
                    deferred = post
                elif g in ("qb", "kb"):
                    cs = CS.next()
                    S_.dma("sp", cs, cs_D[t])
                    cosb = bcm(cs[:, 0:64], NH)
                    sinb = bcm(cs[:, 64:128], NH)
                    xs_ = F4K.next()
                    S_.cp("act", xs_, pp)
                    x1 = xs_.h3()[:, :, 0:64]
                    x2 = xs_.h3()[:, :, 64:128]
                    ta = F4K.next()
                    tb = F4K.next()
                    S_.tt("dve", ta.h3()[:, :, 0:64], x1, cosb, ALU.mult)
                    S_.tt("dve", ta.h3()[:, :, 64:128], x2, sinb, ALU.mult)
                    S_.tt("dve", tb.h3()[:, :, 0:64], x1, sinb, ALU.mult)
                    S_.tt("dve", tb.h3()[:, :, 64:128], x2, cosb, ALU.mult)
                    rot = F4K.next()
                    S_.tt("dve", rot.h3()[:, :, 0:64], ta.h3()[:, :, 0:64], ta.h3()[:, :, 64:128], ALU.subtract)
                    S_.tt("dve", rot.h3()[:, :, 64:128], tb.h3()[:, :, 0:64], tb.h3()[:, :, 64:128], ALU.add)
                    o = B2K.next()
                    S_.tt("dve", o.h3(), rot.h3(), bch(decq if g == "qb" else deck, HD), ALU.mult)
                    S_.dma("pool", (QB_D if g == "qb" else KB_D)[t], o)
                elif g == "vb":
                    o = B2K.next()
                    S_.cp("act", o, pp)
                    S_.dma("pool", VB_D[t], o)
                else:
                    p0 = PS(pb0)
                    kn = B2K.next()
                    junk = F2K.next()
                    ss = SM.next()[:, 0:1]
                    S_.act(junk[:, 0:128], p0[:, 0:128], AF.Square, accum=ss)
                    rs = rstd_from(ss, 1, 1.0 / HD)
                    S_.ts("dve", kn[:, 0:128], p0[:, 0:128], rs, None, ALU.mult)
                    S_.cp("act", kn[:, 128:192], p0[:, 256:320])
                    S_.cp("act", kn[:, 192:256], p0[:, 256:320])
                    S_.cp("act", T(v_t[:, t, 0:128], vbufs[t]), p0[:, 128:256])
                    S_.cp("dve", T(wi_t[:, t, :], wibufs[t]), p0[:, 320:336])

                    def post(kn=kn, t=t):
                        pT = PSbf(4 + t % 2)
                        S_.tr(pT[:, 0:128], kn[:, 0:128], ident)
                        S_.tr(pT[:, 128:256], kn[:, 128:256], ident)
                        S_.ts("dve", T(kt_t[:, t * 128:(t + 1) * 128], kbufs[t]), pT[:, 0:128], gkq, None, ALU.mult)
                        S_.cp("act", T(kit_t[:, t * 128:(t + 1) * 128], kibufs[t]), pT[:, 128:256])
                    if deferred is not None:
                        deferred()
                    deferred = post
            if deferred is not None:
                deferred()
            c0 += ncols

        if upto is not None and upto.startswith('A2'):
            return finish()
        def indexer(qb):
            Iacc = IA[qb % 2]
            nk = qb + 1
            Sc = nk * 128
            qit = QP.next()
            S_.dma("sp", qit, QiT_D[qb])
            qt = QP.next()
            S_.dma("sp", qt, QT_D[qb])
            wq = T(wi_t[:, qb, :], wibufs[qb])
            dgw = DGW.next()
            S_.tt("dve", dgw.h3(NIH), bcm(ident, NIH), bch(wq, 128), ALU.mult)
            for gi, g0 in enumerate(range(0, nk, 4)):
                ge = min(g0 + 4, nk)
                ncl = (ge - g0) * 128
                Icol = Iacc[:, g0 * 128:g0 * 128 + ncl]
                accb = PS(4 + gi % 2)[:, 0:ncl]
                LOOK = 3
                rr = {}
                for hh_ in range(NIH + LOOK):
                    if hh_ < NIH:
                        h = hh_
                        pair, hf = h // 2, h % 2
                        lo_ = hf * 64
                        S_.mm(PS(h % 4)[:, 0:ncl], qit.v(lambda a: a[lo_:lo_ + 64, pair * 128:(pair + 1) * 128]),
                              T(kit_t[lo_:lo_ + 64, g0 * 128:g0 * 128 + ncl], kibufs[g0:ge]), start=True, stop=True)
                    if hh_ >= LOOK:
                        h = hh_ - LOOK
                        pb = PS(h % 4)[:, 0:ncl]
                        r = R16.next()[:, 0:ncl]
                        S_.act(r, pb, AF.Relu)
                        S_.mm(accb, dgw[:, h * 128:(h + 1) * 128], r, start=(h == 0), stop=(h == NIH - 1))
                S_.cp("act", Icol, accb)
            return qt

        def rest(qb, qt):
            Iacc = IA[qb % 2]
            nk = qb + 1
            Sc = nk * 128
            ga = F4K.next()
            S_.dma("sp", ga, GA_D[qb])
            Idiag = Iacc[:, qb * 128:(qb + 1) * 128]
            S_.tt("dve", Idiag, Idiag, negtri, ALU.add)
            lh = LH.next()
            hi = lh[:, 0:1]
            lo = lh[:, 1:2]
            S_.red(hi, Iacc[:, 0:Sc], ALU.max)
            tdb = B2K.next()
            td = T(tdb.ap.bitcast(F32)[:, 0:128], tdb.bufs)
            S_.stt(td, negtri, -2.0, Idiag, ALU.mult, ALU.add)
            S_.red(lo, td, ALU.min)
            if qb > 0:
                lo2 = lh[:, 2:3]
                S_.red(lo2, Iacc[:, 0:qb * 128], ALU.min)
                S_.tt("dve", lo, lo, lo2, ALU.min)
            if Sc > TOPK:
                w0 = lh[:, 3:4]
                S_.tt("dve", w0, hi, lo, ALU.subtract)
                stp = ST.next()
                S_.ts("dve", stp, pow2, w0, None, ALU.mult)
                mid = MID.next()
                S_.tt("dve", mid, lo, stp[:, 0:1], ALU.add)
                Sa = (nk // 2) * 128 if BIS_SPLIT else 0
                for it in range(NBIS):
                    cnt = SM.next()[:, 0:1]
                    sgn = SM.next()[:, 0:1]
                    gst = SM.next()[:, 0:1]
                    if Sa > 0:
                        S_.act(maskT[:, 0:Sa], Iacc[:, 0:Sa], AF.Sign, scale=-1.0, bias=mid, accum=sgn)
                    S_.ts("dve", maskb[:, Sa:Sc], Iacc[:, Sa:Sc], mid, 0.0, ALU.is_ge, ALU.add, accum=cnt)
                    if Sa > 0:
                        S_.stt(cnt, sgn, -0.5, cnt, ALU.mult, ALU.add)
                    S_.ts("dve", gst, cnt, TOPK - 0.5 * Sa - 0.25, stp[:, it:it + 1], ALU.is_ge, ALU.mult)
                    mid2 = MID.next()
                    S_.ts("dve", mid2, gst, mid, stp[:, it + 1:it + 2], ALU.add, ALU.subtract)
                    mid = mid2
                S_.tt("dve", lo, mid, stp[:, NBIS:NBIS + 1], ALU.subtract)
            S_.ts("dve", maskb[:, 0:Sc], Iacc[:, 0:Sc], lo, None, ALU.is_ge)
            if debug:
                S_.dma("pool", dtile(dbgI_d[qb][:, 0:Sc]), Iacc[:, 0:Sc])
                S_.dma("pool", dtile(dbgM_d[qb][:, 0:Sc]), maskb[:, 0:Sc])
                dt_ = SM.next()
                S_.cp("dve", dt_[:, 0:1], lo)
                S_.cp("dve", dt_[:, 1:2], hi)
                S_.dma("pool", dtile(dbgT_d[qb]), dt_)
            for k0 in range(0, nk, 8):
                ke = min(k0 + 8, nk)
                pT = PSbf(7)
                for kb in range(k0, ke):
                    S_.tr(pT[:, (kb - k0) * 128:(kb - k0 + 1) * 128], maskb[:, kb * 128:(kb + 1) * 128], ident)
                S_.cp("act", maskT[:, k0 * 128:ke * 128], pT[:, 0:(ke - k0) * 128])
            def qk(kb, ki):
                pb0 = 0 if ki % 2 == 0 else 2
                KTb = T(kt_t[:, kb * 128:(kb + 1) * 128], kbufs[kb])
                for hh in range(2):
                    S_.mm(PS(pb0 + hh), KTb, qt[:, hh * 512:(hh + 1) * 512], start=True, stop=True)

            korder = [kb_ for kb_ in (qb, qb - 1) if kb_ >= 0] + list(range(0, max(qb - 1, 0)))
            assert len(korder) == nk
            qk(korder[0], 0)
            for ki, kb in enumerate(korder):
                if ki + 1 < nk:
                    qk(korder[ki + 1], ki + 1)
                pp = PS(0, 2) if ki % 2 == 0 else PS(2, 2)
                e_ = B2K.next()
                S_.act(e_, pp, AF.Exp)
                p_ = B2K.next()
                mTb = maskT[:, kb * 128:(kb + 1) * 128]
                if kb >= qb - 1:
                    EB = EB0 if kb == qb else EB1
                    mn = B2K.next()
                    S_.tt("pool", mn.h3(), EB.h3(), bcm(mTb, NH), ALU.mult)
                    S_.tt("dve", p_, e_, mn, ALU.mult)
                else:
                    S_.tt("dve", p_.h3(), e_.h3(), bcm(mTb, NH), ALU.mult)
                Vb = T(v_t[:, kb, :], vbufs[kb])
                for h in range(NH):
                    S_.mm(PS(4 + h // 3)[:, (h % 3) * 129:(h % 3 + 1) * 129], p_[:, h * 128:(h + 1) * 128], Vb,
                          start=(ki == 0 and h % 3 == 0), stop=(ki == nk - 1), sgc=True)
            rden = SM.next()
            o_ = F4K.next()
            for bk in range(3):
                h0 = bk * 3
                nh = min(3, NH - h0)
                pv = PS(4 + bk).v(lambda a: a[:, 0:nh * 129].rearrange("p (h d) -> p h d", d=129))
                S_.recip(rden[:, h0:h0 + nh].v(lambda a: a.unsqueeze(2)), pv[:, :, 128:129])
                S_.tt("dve", o_[:, h0 * 128:(h0 + nh) * 128].h3(nh), pv[:, :, 0:128], bch(rden[:, h0:h0 + nh], HD), ALU.mult)
            y_ = B2K.next()
            S_.tt("dve", y_, o_, ga, ALU.mult)
            def postB(y_=y_, qb=qb):
                pT = PSbf(7)
                for h in range(NH):
                    S_.tr(pT[:, h * 128:(h + 1) * 128], y_[:, h * 128:(h + 1) * 128], ident)
                yo = B2K.next()
                S_.cp("act", yo, pT)
                S_.dma("pool", yTa_D[qb], yo)
            return postB

        qts = {0: indexer(0)}
        defB = None
        for qb in range(NTT):
            if qb + 1 < NTT:
                qts[qb + 1] = indexer(qb + 1)
            if defB is not None:
                defB()
            defB = rest(qb, qts.pop(qb))
        defB()

        if upto == 'B':
            return finish()
        wgenD = wgroups(wout_d, l, [(0, 1024), (1024, 1024)])
        slabsD0 = next(wgenD)
        S_.op("dve", lambda e: e.memset(R.ap, 0.0), [], [R])
        S_.op("dve", lambda e: e.memset(Rbf.ap, 0.0), [], [Rbf])
        defC = None
        for c in range(NTT):
            q_ = B2K.next()
            k_ = B2K.next()
            v_ = B2K.next()
            gb = F4K.next()
            S_.dma("sp", q_, QB_D[c])
            S_.dma("sp", k_, KB_D[c])
            S_.dma("sp", v_, VB_D[c])
            S_.dma("sp", gb, GB_D[c])
            pq, pk = PSbf(6), PSbf(7)
            for h in range(NH):
                S_.tr(pq[:, h * 128:(h + 1) * 128], q_[:, h * 128:(h + 1) * 128], ident)
            for h in range(NH):
                S_.tr(pk[:, h * 128:(h + 1) * 128], k_[:, h * 128:(h + 1) * 128], ident)
            qT = B2K.next()
            kT = B2K.next()
            S_.cp("act", qT, pq)
            S_.cp("dve", kT, pk)
            for h in range(NH):
                hs = slice(h * 128, (h + 1) * 128)
                S_.mm(PS(h // 4)[:, (h % 4) * 128:(h % 4 + 1) * 128], kT[:, hs], qT[:, hs],
                      start=(h % 4 == 0), stop=True, sgc=True)
            if defC is not None:
                defC()
                defC = None
            aT = B2K.next()
            S_.tt("dve", aT.h3(), PS(0, 2).h3(), bcm(causT, NH), ALU.mult)
            for h in range(NH):
                hs = slice(h * 128, (h + 1) * 128)
                po = PS(4 + h // 4)[:, (h % 4) * 128:(h % 4 + 1) * 128]
                S_.mm(po, aT[:, hs], v_[:, hs], start=(h % 4 == 0), stop=False, sgc=True)
                S_.mm(po, qT[:, hs], Rbf[:, hs], start=False, stop=True, sgc=True)
            for h in range(NH):
                hs = slice(h * 128, (h + 1) * 128)
                S_.mm(PS(2 + h // 4)[:, (h % 4) * 128:(h % 4 + 1) * 128], k_[:, hs], v_[:, hs],
                      start=(h % 4 == 0), stop=True, sgc=True)
            rt = F4K.next()
            S_.tt("dve", rt, PS(2, 2), R, ALU.add)
            S_.tt("dve", R, rt, GCt, ALU.mult)
            S_.cp("act", Rbf, R)
            po = PS(4, 2)
            s1 = SM.next()
            s2 = SM.next()
            S_.red(s1, po.h3(), ALU.add)
            sq = F4K.next()
            S_.act(sq, po, AF.Square)
            S_.red(s2, sq.h3(), ALU.add)
            mean = SM.next()
            S_.ts("dve", mean, s1, 1.0 / HD, None, ALU.mult)
            m2 = SM.next()
            S_.tt("dve", m2, mean, mean, ALU.mult)
            var = SM.next()
            S_.stt(var, s2, 1.0 / HD, m2, ALU.mult, ALU.subtract)
            rs = rstd_from(var, 8, 1.0)
            xc = F4K.next()
            S_.tt("dve", xc.h3(), po.h3(), bch(mean, HD), ALU.subtract)
            S_.tt("dve", xc.h3(), xc.h3(), bch(rs, HD), ALU.mult)
            S_.tt("dve", xc, xc, RG, ALU.mult)
            yb = B2K.next()
            S_.tt("dve", yb, xc, gb, ALU.mult)
            def postC(yb=yb, c=c):
                pT = PSbf(6)
                for h in range(NH):
                    S_.tr(pT[:, h * 128:(h + 1) * 128], yb[:, h * 128:(h + 1) * 128], ident)
                yo = B2K.next()
                S_.cp("act", yo, pT)
                S_.dma("pool", yTb_D[c], yo)
            defC = postC
        defC()

        if upto == 'C':
            return finish()
        g_bc = F8K[0]
        S_.dma("sp", g_bc, T(modraw_d[l, 2 * D:3 * D].partition_broadcast(128), [modraw_B]))
        for half_i in range(2):
            slabs = slabsD0 if half_i == 0 else next(wgenD)
            for t in range(NTT):
                yt = B4K.next()
                S_.dma("sp", T(yt.ap[:, 0:1024], yt.bufs), yTa_D[t])
                S_.dma("sp", T(yt.ap[:, 1024:2048], yt.bufs), yTb_D[t])
                pb0 = 0 if t % 2 == 0 else 2
                for i, slab in enumerate(slabs):
                    for kc in range(16):
                        S_.mm(PS(pb0 + i), yt[:, kc * 128:(kc + 1) * 128], slab.v(lambda a: a[:, kc, :]),
                              start=(kc == 0), stop=(kc == 15))
                cs_ = slice(half_i * 1024, (half_i + 1) * 1024)
                xh = F4K.next()
                src = xsrc[t] if half_i == 0 or True else None
                S_.dma("sp", xh, T(src.ap[:, cs_], src.bufs))
                tmp = F4K.next()
                S_.tt("dve", tmp, PS(pb0, 2), g_bc[:, cs_], ALU.mult)
                S_.tt("pool", tmp, tmp, xh, ALU.add)
                dst = xdst[t]
                S_.dma("pool", T(dst.ap[:, cs_], dst.bufs), tmp)

    return finish()


_CONST = None


def _consts():
    global _CONST
    if _CONST is not None:
        return _CONST
    i = np.arange(128)
    ident = np.eye(128, dtype=np.float32)
    causT = (i[:, None] <= i[None, :]).astype(np.float32)
    negtri = np.where(i[None, :] <= i[:, None], 0.0, NEG).astype(np.float32)
    invf = (10000.0 ** (-np.arange(64, dtype=np.float32) / np.float32(64))).astype(np.float32)
    h = np.arange(8, dtype=np.float64)
    log_g = np.log1p(-np.exp2(-5.0 - h))
    decq = np.exp(log_g[None, :] * (i[:, None] + 1.0)).astype(np.float32)
    deck = (np.exp(-log_g[None, :] * (i[:, None] + 1.0)) * (128.0 ** -0.5)).astype(np.float32)
    gc = np.repeat(np.exp(log_g * 128.0), 128).astype(np.float32)
    rel0 = i[None, :] - i[:, None]
    idx0 = _t5_bucket(rel0)
    idx1 = _t5_bucket(rel0 + 128)
    pow2 = np.ascontiguousarray(np.broadcast_to((2.0 ** -(np.arange(32) + 1.0))[None, :], (128, 32))).astype(np.float32)
    _CONST = dict(pow2=pow2, ident=ident, causT=causT, negtri=negtri, invf=invf, decq=decq, deck=deck, gc=gc,
                  idx0=idx0, idx1=idx1)
    return _CONST


def make_in_map(b, x, c, positions, rel_bias, norm_gain, w_mod, b_mod, w_in_p, q_norm_gain, k_norm_gain,
                ret_norm_gain, w_out, NL, NTT):
    C = _consts()
    S = NTT * 128
    bt0 = np.ascontiguousarray(rel_bias[C["idx0"]].transpose(0, 2, 1)).reshape(128, 1024)
    bt1 = np.ascontiguousarray(rel_bias[C["idx1"]].transpose(0, 2, 1)).reshape(128, 1024)
    b31 = np.ascontiguousarray(np.broadcast_to(rel_bias[31][None, :, None], (128, 8, 128))).reshape(128, 1024)
    return {
        "x": np.ascontiguousarray(x[b, :S]),
        "c": np.ascontiguousarray(c[b].reshape(16, 128).T),
        "positions": np.ascontiguousarray(positions[b, :S].reshape(NTT, 128).T),
        "norm_gain": norm_gain[:NL], "w_mod": w_mod[:NL], "b_mod": b_mod[:NL], "w_in": w_in_p[:NL],
        "q_norm_gain": np.ascontiguousarray(q_norm_gain[:NL, :, None]),
        "k_norm_gain": np.ascontiguousarray(k_norm_gain[:NL, :, None]),
        "ret_norm_gain": ret_norm_gain[:NL], "w_out": w_out[:NL],
        "ident": C["ident"], "causT": C["causT"], "negtri": C["negtri"], "invf": C["invf"],
        "decq": C["decq"], "deck": C["deck"], "gc": C["gc"], "pow2": C["pow2"],
        "bt0": bt0.astype(np.float32), "bt1": bt1.astype(np.float32), "b31": b31.astype(np.float32),
    }


def kernel(x, c, positions, rel_bias, norm_gain, w_mod, b_mod, w_in, q_norm_gain, k_norm_gain,
           ret_norm_gain, w_out):
    x = np.asarray(x, np.float32)
    B = x.shape[0]
    w_in_p = np.ascontiguousarray(np.asarray(w_in, np.float32)[:, :, PERM])
    nc = build()
    args = [np.asarray(a) for a in (c, positions, rel_bias, norm_gain, w_mod, b_mod)]
    rest = [np.asarray(a, np.float32) for a in (q_norm_gain, k_norm_gain, ret_norm_gain, w_out)]
    in_maps = [make_in_map(b, x, args[0].astype(np.float32), args[1].astype(np.int32), args[2].astype(np.float32),
                           args[3].astype(np.float32), args[4].astype(np.float32), args[5].astype(np.float32),
                           w_in_p, *rest, DEPTH, SEQ // 128) for b in range(B)]
    res = run_bass_kernel_spmd(nc, in_maps, core_ids=list(range(B)))
    return np.stack([np.asarray(r["out"], np.float32) for r in res.results], axis=0)
```

```python
import math
from contextlib import ExitStack
import numpy as np
import concourse.bass as bass
import concourse.mybir as mybir
from concourse.bass_utils import run_bass_kernel_spmd

F32 = mybir.dt.float32
BF16 = mybir.dt.bfloat16
I32 = mybir.dt.int32
AF = mybir.ActivationFunctionType
ALU = mybir.AluOpType
AX = mybir.AxisListType

D = 2048
HD = 128
NH = 8
NIH = 16
DEPTH = 4
SEQ = 4096
EPS = 1e-6
NEG = -1.0e30
N_IN = 7504
GROUPS = ["qa", "ga", "qi", "qb", "kb", "vb", "gb", "misc"]
ORIG = {"qa": (0, 1024), "ka": (1024, 1152), "va": (1152, 1280), "ga": (1280, 2304),
        "qi": (2304, 3328), "ki": (3328, 3392), "wi": (3392, 3408), "qb": (3408, 4432),
        "kb": (4432, 5456), "vb": (5456, 6480), "gb": (6480, 7504)}
PERM = np.concatenate([np.arange(*ORIG[k]) for k in
                       ["qa", "ga", "qi", "qb", "kb", "vb", "gb", "ka", "va", "ki", "wi"]])
NBIS = 12
import os as _os
BIS_SPLIT = int(_os.environ.get('BIS_SPLIT', '0'))


class Buf:
    __slots__ = ("w", "r", "x")

    def __init__(self, x=False):
        self.w = None
        self.r = {}
        self.x = x


class T:
    def __init__(self, ap, bufs):
        self.ap = ap
        self.bufs = bufs if isinstance(bufs, list) else [bufs]

    def __getitem__(self, k):
        return T(self.ap[k], self.bufs)

    def v(self, f):
        return T(f(self.ap), self.bufs)

    def h3(self, h=NH):
        return T(self.ap.rearrange("p (h d) -> p h d", h=h), self.bufs)


def bch(t, n):
    return T(t.ap.unsqueeze(2).broadcast_to([t.ap.shape[0], t.ap.shape[1], n]), t.bufs)


def bcm(t, h):
    return T(t.ap.unsqueeze(1).broadcast_to([t.ap.shape[0], h, t.ap.shape[1]]), t.bufs)


class Rot:
    def __init__(self, items):
        self.items = items
        self.i = 0

    def next(self):
        t = self.items[self.i]
        self.i = (self.i + 1) % len(self.items)
        return t


NDS = 24


class Sched:
    def __init__(self, nc, stack):
        self.nc = nc
        self.names = ["pe", "act", "dve", "pool", "sp"]
        self.ops = {k: [] for k in self.names}
        self.esem = {k: stack.enter_context(nc.semaphore("es_" + k)) for k in ("pe", "act", "dve", "pool")}
        self.cnt = {k: 0 for k in self.esem}
        self.seen = {k: {} for k in self.names}
        self.dsems = [stack.enter_context(nc.semaphore("ds%d" % i)) for i in range(NDS)]
        self.dcnt = [0] * NDS
        self.dnext = 0

    def _need(self, e, need, ev):
        if ev is None:
            return
        k, v = ev
        if e == "pe" and k == "pe":
            return
        if self.seen[e].get(k, 0) >= v:
            return
        if need.get(k, 0) < v:
            need[k] = v

    def _deps(self, e, reads, writes):
        need = {}
        for t in reads:
            for b in t.bufs:
                self._need(e, need, b.w)
                if b.x:
                    for k, v in b.r.items():
                        if k != e:
                            self._need(e, need, (k, v))
        for t in writes:
            for b in t.bufs:
                self._need(e, need, b.w)
                for k, v in b.r.items():
                    self._need(e, need, (k, v))
        for k, v in need.items():
            self.seen[e][k] = v
        return list(need.items())

    def op(self, e, fn, reads, writes):
        need = self._deps(e, reads, writes)
        self.cnt[e] += 1
        v = self.cnt[e]
        self.ops[e].append((need, fn, e))
        for t in reads:
            for b in t.bufs:
                b.r[e] = v
        for t in writes:
            for b in t.bufs:
                b.w = (e, v)
                b.r = {}

    def dma(self, q, out, in_):
        i = self.dnext
        self.dnext = (i + 1) % NDS
        key = ("d", i)
        need = self._deps(q, [in_], [out])
        if self.dcnt[i] > 0 and self.seen[q].get(key, 0) < self.dcnt[i]:
            need.append((key, self.dcnt[i]))
            self.seen[q][key] = self.dcnt[i]
        self.dcnt[i] += 16
        v = self.dcnt[i]
        oa, ia = out.ap, in_.ap
        self.ops[q].append((need, lambda eng: eng.dma_start(out=oa, in_=ia), key))
        for b in in_.bufs:
            b.r[key] = v
        for b in out.bufs:
            b.w = (key, v)
            b.r = {}

    def semh(self, k):
        return self.dsems[k[1]] if isinstance(k, tuple) else self.esem[k]

    def emit(self):
        fin = [(("d", i), c) for i, c in enumerate(self.dcnt) if c > 0]
        fin += [(k, c) for k, c in self.cnt.items() if c > 0]
        nc = self.nc

        def mk(name):
            def body(eng):
                for need, fn, inc in self.ops[name]:
                    for k, v in need:
                        eng.wait_ge(self.semh(k), v)
                    ins = fn(eng)
                    if isinstance(inc, tuple):
                        ins.then_inc(self.dsems[inc[1]], 16)
                    else:
                        ins.then_inc(self.esem[inc], 1)
                if name == "sp":
                    for k, v in fin:
                        eng.wait_ge(self.semh(k), v)
            return body

        with nc.Block() as block:
            block.tensor(mk("pe"))
            block.scalar(mk("act"))
            block.vector(mk("dve"))
            block.gpsimd(mk("pool"))
            block.sync(mk("sp"))

    def mm(self, out, lhsT, rhs, start, stop, sgc=False):
        o, l, r = out.ap, lhsT.ap, rhs.ap
        self.op("pe", lambda e: e.matmul(o, l, r, start=start, stop=stop, skip_group_check=sgc),
                [lhsT, rhs], [out])

    def tr(self, out, in_, ident):
        o, i, d = out.ap, in_.ap, ident.ap
        self.op("pe", lambda e: e.transpose(o, i, d), [in_, ident], [out])

    def act(self, out, in_, func, scale=None, bias=None, accum=None):
        o, i = out.ap, in_.ap
        kw = {}
        rd = [in_]
        wr = [out]
        if scale is not None:
            if isinstance(scale, T):
                kw["scale"] = scale.ap
                rd.append(scale)
            else:
                kw["scale"] = scale
        if bias is not None:
            if isinstance(bias, T):
                kw["bias"] = bias.ap
                rd.append(bias)
            else:
                kw["bias"] = bias
        if accum is not None:
            kw["accum_out"] = accum.ap
            wr.append(accum)
        self.op("act", lambda e: e.activation(o, i, func, **kw), rd, wr)

    def ts(self, eng, out, in0, s1, s2, op0, op1=None, accum=None):
        o, i = out.ap, in0.ap
        rd = [in0]
        wr = [out]
        a1 = s1
        a2 = s2
        if isinstance(s1, T):
            a1 = s1.ap
            rd.append(s1)
        if isinstance(s2, T):
            a2 = s2.ap
            rd.append(s2)
        kw = {}
        if op1 is not None:
            kw["op1"] = op1
        if accum is not None:
            kw["accum_out"] = accum.ap
            wr.append(accum)
        self.op(eng, lambda e: e.tensor_scalar(o, i, a1, a2, op0, **kw), rd, wr)

    def tt(self, eng, out, in0, in1, op):
        o, a, b = out.ap, in0.ap, in1.ap
        self.op(eng, lambda e: e.tensor_tensor(o, a, b, op), [in0, in1], [out])

    def stt(self, out, in0, scalar, in1, op0, op1):
        o, a, b = out.ap, in0.ap, in1.ap
        rd = [in0, in1]
        s = scalar
        if isinstance(scalar, T):
            s = scalar.ap
            rd.append(scalar)
        self.op("dve", lambda e: e.scalar_tensor_tensor(o, a, s, b, op0, op1), rd, [out])

    def red(self, out, in_, op, axis=AX.X):
        o, i = out.ap, in_.ap
        self.op("dve", lambda e: e.tensor_reduce(o, i, axis, op), [in_], [out])

    def cp(self, eng, out, in_):
        o, i = out.ap, in_.ap
        if eng == "act":
            self.op("act", lambda e: e.copy(o, i), [in_], [out])
        else:
            self.op(eng, lambda e: e.tensor_copy(o, i), [in_], [out])

    def recip(self, out, in_):
        o, i = out.ap, in_.ap
        self.op("dve", lambda e: e.reciprocal(o, i), [in_], [out])


def _t5_bucket(rel):
    rel = np.maximum(rel, 0)
    relf = np.maximum(rel, 1).astype(np.float32)
    large = 16 + (np.log(relf / np.float32(16)) / np.float32(math.log(128 / 16)) * np.float32(16)).astype(np.int32)
    large = np.minimum(large, 31)
    return np.where(rel < 16, rel, large)


def build(NL=DEPTH, NTT=32, TOPK=256, debug=False, upto=None):
    S = NTT * 128
    nc = bass.Bass("TRN2", target_bir_lowering=False)
    st = ExitStack()
    kout = "ExternalOutput" if debug else "Internal"

    def din(name, shape, dt=F32):
        return nc.dram_tensor(name, shape, dt, kind="ExternalInput").ap()

    def dsc(name, shape, dt=F32, kind=None):
        return nc.dram_tensor(name, shape, dt, kind=kind or kout).ap()

    x_d = din("x", [S, D])
    c_d = din("c", [128, 16])
    pos_d = din("positions", [128, NTT], I32)
    ng_d = din("norm_gain", [NL, D])
    wmod_d = din("w_mod", [NL, D, 3 * D])
    bmod_d = din("b_mod", [NL, 3 * D])
    win_d = din("w_in", [NL, D, N_IN])
    qg_d = din("q_norm_gain", [NL, 128, 1])
    kg_d = din("k_norm_gain", [NL, 128, 1])
    rg_d = din("ret_norm_gain", [NL, 1024])
    wout_d = din("w_out", [NL, D, D])
    ident_d = din("ident", [128, 128])
    caus_d = din("causT", [128, 128])
    negtri_d = din("negtri", [128, 128])
    invf_d = din("invf", [64])
    decq_d = din("decq", [128, 8])
    deck_d = din("deck", [128, 8])
    gc_d = din("gc", [1024])
    pow2_d = din("pow2", [128, 32])
    bt0_d = din("bt0", [128, 1024])
    bt1_d = din("bt1", [128, 1024])
    b31_d = din("b31", [128, 1024])
    out_d = nc.dram_tensor("out", [S, D], F32, kind="ExternalOutput").ap()

    modraw_d = dsc("modraw", [NL, 3 * D])
    cs_d = dsc("cs", [NTT, 128, 128])
    hT_d = dsc("hT", [NTT, 128, 2048], BF16)
    QT_d = dsc("QT", [NTT, 128, 1024], BF16)
    QiT_d = dsc("QiT", [NTT, 128, 1024], BF16)
    GA_d = dsc("GA", [NTT, 128, 1024])
    QB_d = dsc("QB", [NTT, 128, 1024], BF16)
    KB_d = dsc("KB", [NTT, 128, 1024], BF16)
    VB_d = dsc("VB", [NTT, 128, 1024], BF16)
    GB_d = dsc("GB", [NTT, 128, 1024])
    yT_d = dsc("yT", [NTT, 128, 2048], BF16)
    xs_d = dsc("xs", [S, D], kind="Internal")
    if debug:
        dbgI_d = dsc("dbgI", [NTT, 128, S])
        dbgT_d = dsc("dbgT", [NTT, 128, 8])
        dbgM_d = dsc("dbgM", [NTT, 128, S], BF16)
        dbgO_d = dsc("dbgO", [NTT, 128, 1032])

    def sb(name, shape, dt=F32):
        return st.enter_context(nc.sbuf_tensor("s_" + name, shape, dt))

    ps_t = st.enter_context(nc.psum_tensor("ps", [128, 4096], F32))
    S_ = Sched(nc, st)

    def finish():
        print("sbuf bytes remaining", nc.sbuf_bytes_remaining)
        S_.emit()
        st.close()
        return nc
    psb = [Buf(True) for _ in range(8)]

    def PS(b, n=1):
        return T(ps_t[:, b * 512:(b + n) * 512], psb[b:b + n])

    def PSbf(b):
        return T(ps_t[:, b * 512:(b + 1) * 512].bitcast(BF16), [psb[b]])

    def tile(name, shape, dt=F32):
        return T(sb(name, shape, dt)[:], Buf())

    def dtile(ap):
        return T(ap, Buf())

    arena = sb("arena8", [128, 5, 2048], F32)
    F8K = [T(arena[:, i, :], Buf()) for i in range(5)]
    Iacc = T(arena[:, 0:2, :].rearrange("p a b -> p (a b)"), F8K[0].bufs + F8K[1].bufs)
    maskb = T(arena[:, 2, :].bitcast(BF16), F8K[2].bufs)
    maskT = T(arena[:, 3, :].bitcast(BF16), F8K[3].bufs)
    f8k = Rot(F8K[2:5])
    wsl_t = sb("wslab", [128, 3, 16, 512], BF16)
    WSL = Rot([T(wsl_t[:, i, :, :], Buf()) for i in range(3)])
    b4k_t = sb("b4k", [128, 4, 2048], BF16)
    B4K = Rot([T(b4k_t[:, i, :], Buf()) for i in range(4)])
    IaccB = T(b4k_t[:].rearrange("p a b -> p (a b)").bitcast(F32), [b for t_ in B4K.items for b in t_.bufs])
    IA = [Iacc, IaccB]
    f4k_t = sb("f4k", [128, 4, 1024], F32)
    F4K = Rot([T(f4k_t[:, i, :], Buf()) for i in range(4)])
    b2k_t = sb("b2k", [128, 8, 1024], BF16)
    B2K = Rot([T(b2k_t[:, i, :], Buf()) for i in range(8)])
    qp_t = sb("qp", [128, 4, 1024], BF16)
    QP = Rot([T(qp_t[:, i, :], Buf()) for i in range(4)])
    st_t = sb("stp", [128, 2, 32], F32)
    ST = Rot([T(st_t[:, i, :], Buf()) for i in range(2)])
    mid_t = sb("mid", [128, 4, 1], F32)
    MID = Rot([T(mid_t[:, i, :], Buf()) for i in range(4)])
    pow2 = tile("pow2", [128, 32])
    dgw_t = sb("dgw", [128, 1, 2048], BF16)
    DGW = Rot([T(dgw_t[:, i, :], Buf()) for i in range(1)])
    lh_t = sb("lohi", [128, 2, 4], F32)
    LH = Rot([T(lh_t[:, i, :], Buf()) for i in range(2)])
    f2k_t = sb("f2k", [128, 3, 512], F32)
    F2K = Rot([T(f2k_t[:, i, :], Buf()) for i in range(3)])
    R16P = Rot([T(f2k_t[:, i, :].bitcast(BF16), F2K.items[i].bufs) for i in range(3)])
    sm_t = sb("small", [128, 48, 8], F32)
    SM = Rot([T(sm_t[:, i, :], Buf()) for i in range(48)])
    cs_t = sb("cst", [128, 3, 128], F32)
    CS = Rot([T(cs_t[:, i, :], Buf()) for i in range(3)])

    kt_t = sb("KT", [128, S], BF16)
    kbufs = [Buf() for _ in range(NTT)]
    v_t = sb("V", [128, NTT, 129], BF16)
    vbufs = [Buf() for _ in range(NTT)]
    kit_t = sb("KiT", [128, S], BF16)
    kibufs = [Buf() for _ in range(NTT)]
    wi_t = sb("WI", [128, NTT, 16], F32)
    wibufs = [Buf() for _ in range(NTT)]
    R = tile("R", [128, 1024])
    Rbf = tile("Rbf", [128, 1024], BF16)

    ident = tile("ident", [128, 128], BF16)
    causT = tile("causTb", [128, 128], BF16)
    negtri = tile("negtri", [128, 128])
    EB0 = tile("EB0", [128, 1024], BF16)
    EB1 = tile("EB1", [128, 1024], BF16)
    GCt = tile("GCt", [128, 1024])
    RG = tile("RGt", [128, 1024])
    decq = tile("decq", [128, 8])
    deck = tile("deck", [128, 8])
    half = tile("half", [128, 1])
    ones = tile("ones", [128, 2], BF16)
    gkq = tile("gkq", [128, 1])
    gk2 = tile("gk2", [128, 1])
    cact = tile("cact", [128, 16])
    cin = tile("cin", [128, 16])
    modsb = tile("modsb", [1, 512])
    bmsb = tile("bmsb", [1, 512])

    def dbufs(ap3):
        return [T(ap3[i], Buf()) for i in range(ap3.shape[0])]

    hT_D = dbufs(hT_d)
    QT_D = dbufs(QT_d)
    QiT_D = dbufs(QiT_d)
    GA_D = dbufs(GA_d)
    QB_D = dbufs(QB_d)
    KB_D = dbufs(KB_d)
    VB_D = dbufs(VB_d)
    GB_D = dbufs(GB_d)
    cs_D = dbufs(cs_d)
    yTa_D = [T(yT_d[i][:, 0:1024], Buf()) for i in range(NTT)]
    yTb_D = [T(yT_d[i][:, 1024:2048], Buf()) for i in range(NTT)]
    xs_D = [T(xs_d[i * 128:(i + 1) * 128, :], Buf()) for i in range(NTT)]
    xin_D = [T(x_d[i * 128:(i + 1) * 128, :], Buf()) for i in range(NTT)]
    out_D = [T(out_d[i * 128:(i + 1) * 128, :], Buf()) for i in range(NTT)]
    modraw_B = Buf()

    def cst(ap):
        return T(ap, [])

    S_.dma("pool", ident, cst(ident_d))
    S_.dma("pool", causT, cst(caus_d))
    S_.dma("sp", negtri, cst(negtri_d))
    S_.dma("sp", decq, cst(decq_d))
    S_.dma("sp", deck, cst(deck_d))
    S_.dma("sp", GCt, cst(gc_d.partition_broadcast(128)))
    S_.dma("sp", cin, cst(c_d))
    S_.dma("sp", pow2, cst(pow2_d))
    S_.op("dve", lambda e: e.memset(half.ap, 0.5), [], [half])
    S_.op("dve", lambda e: e.memset(ones.ap, 1.0), [], [ones])
    S_.op("dve", lambda e: e.memset(v_t[:, :, 128:129], 1.0), [], [T(v_t[:, :, 128:129], vbufs)])
    for (bd, EB, diag) in ((bt0_d, EB0, True), (bt1_d, EB1, False)):
        a = F4K.next()
        b = F4K.next()
        S_.dma("sp", a, cst(bd))
        S_.dma("sp", b, cst(b31_d))
        S_.tt("dve", a, a, b, ALU.subtract)
        if diag:
            S_.act(b, a, AF.Exp)
            S_.tt("dve", EB.h3(), b.h3(), bcm(causT, NH), ALU.mult)
        else:
            S_.act(EB, a, AF.Exp)
    if upto == 'p0a':
        return finish()
    posi = T(sb("posi", [128, NTT], I32)[:], Buf())
    posf = tile("posf", [128, NTT])
    invf = tile("invf", [128, 64])
    S_.dma("sp", posi, cst(pos_d))
    S_.dma("sp", invf, cst(invf_d.partition_broadcast(128)))
    S_.cp("dve", posf, posi)
    TWO_PI = 2.0 * math.pi
    C1 = 6.28125
    C2 = TWO_PI - C1
    for t0 in range(0, NTT, 8):
        nt = min(8, NTT - t0)
        fk = F4K.items
        ang = fk[0][:, 0:nt * 64]
        nn = fk[1][:, 0:nt * 64]
        ni = T(fk[2].ap.bitcast(I32)[:, 0:nt * 64], fk[2].bufs)
        cst_t = fk[3]
        m = fk[2][:, 0:nt * 64]
        a3 = ang.v(lambda a: a.rearrange("p (t j) -> p t j", j=64))
        S_.tt("dve", a3, bch(posf[:, t0:t0 + nt], 64), bcm(invf, nt), ALU.mult)
        S_.ts("dve", nn, ang, 1.0 / TWO_PI, None, ALU.mult)
        S_.cp("dve", ni, nn)
        S_.cp("dve", nn, ni)
        S_.stt(ang, nn, -C1, ang, ALU.mult, ALU.add)
        S_.stt(ang, nn, -C2, ang, ALU.mult, ALU.add)
        for which, shift in ((1, 0.0), (0, 0.5 * math.pi)):
            r = nn
            S_.ts("dve", r, ang, shift, None, ALU.add)
            S_.ts("dve", m, r, math.pi, -TWO_PI, ALU.is_gt, ALU.mult)
            S_.tt("dve", r, r, m, ALU.add)
            S_.ts("dve", m, r, -math.pi, TWO_PI, ALU.is_lt, ALU.mult)
            S_.tt("dve", r, r, m, ALU.add)
            S_.ts("dve", r, r, math.pi, -math.pi, ALU.min, ALU.max)
            o = T(cst_t.ap[:, 0:nt * 128].rearrange("p (t c) -> p t c", c=128)[:, :, which * 64:(which + 1) * 64],
                  cst_t.bufs)
            S_.act(o, r.v(lambda a: a.rearrange("p (t j) -> p t j", j=64)), AF.Sin)
        for i in range(nt):
            S_.dma("pool", cs_D[t0 + i], cst_t[:, i * 128:(i + 1) * 128])
    if upto == 'p0b':
        return finish()
    S_.act(cact, cin, AF.Silu)
    for l in range(NL):
        for n in range(12):
            pb = PS(n % 4)
            for q4 in range(4):
                wst = f8k.next()
                w3 = wst.v(lambda a: a.rearrange("p (k n) -> p k n", k=4))
                S_.dma("sp" if q4 % 2 == 0 else "act", w3,
                       cst(wmod_d[l, q4 * 512:(q4 + 1) * 512, n * 512:(n + 1) * 512].rearrange("(k p) n -> p k n", p=128)))
                for k in range(4):
                    kc = q4 * 4 + k
                    S_.mm(pb[0:1, :], cact[:, kc:kc + 1], w3[:, k, :], start=(kc == 0), stop=(kc == 15))
            S_.dma("sp", bmsb, cst(bmod_d[l:l + 1, n * 512:(n + 1) * 512]))
            S_.tt("dve", modsb, pb[0:1, :], bmsb, ALU.add)
            S_.dma("pool", T(modraw_d[l:l + 1, n * 512:(n + 1) * 512], [modraw_B]), modsb)

    if upto == 'p0':
        return finish()
    def load_w_half(wd, l, c0, ncol):
        slab = WSL.next()
        for hh in range(2):
            S_.dma("pool", slab.v(lambda a: a[:, hh * 8:(hh + 1) * 8, 0:ncol]),
                   cst(wd[l, hh * 1024:(hh + 1) * 1024, c0:c0 + ncol].rearrange("(k p) n -> p k n", p=128)))
        return slab

    def wgroups(wd, l, specs):
        pre = None
        for gi, (c0, ncols) in enumerate(specs):
            nh = (ncols + 511) // 512
            slabs = []
            for i in range(nh):
                if i == 0 and pre is not None:
                    slabs.append(pre)
                else:
                    slabs.append(load_w_half(wd, l, c0 + i * 512, min(512, ncols - i * 512)))
            pre = None
            if gi + 1 < len(specs):
                n0, nn = specs[gi + 1]
                pre = load_w_half(wd, l, n0, min(512, nn))
            yield slabs

    def rstd_from(ss, n, inv_n):
        a = SM.next()[:, 0:n]
        S_.ts("dve", a, ss, inv_n, EPS, ALU.mult, ALU.add)
        b = SM.next()[:, 0:n]
        S_.act(b, a, AF.Sqrt)
        c = SM.next()[:, 0:n]
        S_.recip(c, b)
        return c

    for l in range(NL):
        xsrc = xin_D if l == 0 else xs_D
        xdst = out_D if l == NL - 1 else xs_D
        A_bc, sh_bc = F8K[0], F8K[1]
        tmpg = f8k.next()
        S_.dma("sp", A_bc, T(modraw_d[l, D:2 * D].partition_broadcast(128), [modraw_B]))
        S_.dma("sp", tmpg, cst(ng_d[l].partition_broadcast(128)))
        S_.stt(A_bc, A_bc, 1.0, tmpg, ALU.add, ALU.mult)
        S_.dma("sp", sh_bc, T(modraw_d[l, 0:D].partition_broadcast(128), [modraw_B]))
        S_.dma("sp", gkq, cst(qg_d[l]))
        S_.dma("sp", gk2, cst(kg_d[l]))
        S_.stt(gkq, gkq, HD ** -0.5, gk2, ALU.mult, ALU.mult)
        S_.dma("sp", RG, cst(rg_d[l].partition_broadcast(128)))

        gspecs = []
        c0 = 0
        for g in GROUPS:
            ncols = 1024 if g != "misc" else 336
            gspecs.append((c0, ncols))
            c0 += ncols
        wgen = wgroups(win_d, l, gspecs)
        slabs_first = next(wgen)
        def a1_stage1(t):
            xt = f8k.next()
            S_.dma("sp", xt, xsrc[t])
            junk = B4K.next()
            ss = SM.next()[:, 0:1]
            S_.act(junk, xt, AF.Square, accum=ss)
            rs = rstd_from(ss, 1, 1.0 / D)
            tmp = f8k.next()
            S_.stt(tmp, xt, rs, A_bc, ALU.mult, ALU.mult)
            hb = B4K.next()
            S_.tt("dve", hb, tmp, sh_bc, ALU.add)
            return hb

        def a1_stage2(t, hb):
            for hh in range(2):
                pT = PSbf(6 + hh)
                for k in range(8):
                    kc = hh * 8 + k
                    S_.tr(pT[:, k * 128:(k + 1) * 128], hb[:, kc * 128:(kc + 1) * 128], ident)
            hT = B4K.next()
            S_.cp("act", hT[:, 0:1024], PSbf(6))
            S_.cp("dve", hT[:, 1024:2048], PSbf(7))
            S_.dma("pool", hT_D[t], hT)

        hbs = {0: a1_stage1(0)}
        for t in range(NTT):
            if t + 1 < NTT:
                hbs[t + 1] = a1_stage1(t + 1)
            a1_stage2(t, hbs.pop(t))

        if upto == 'A1':
            return finish()
        c0 = 0
        a2i = 0
        for gidx, g in enumerate(GROUPS):
            ncols = 1024 if g != "misc" else 336
            slabs = slabs_first if gidx == 0 else next(wgen)
            deferred = None
            if upto == 'A2w':
                c0 += ncols
                continue
            for t in range(NTT):
                hT = B4K.next()
                S_.dma("sp", hT, hT_D[t])
                pb0 = (0, 2, 6)[a2i % 3]
                a2i += 1
                pp = PS(pb0, 2)
                for i, slab in enumerate(slabs):
                    nc_i = min(512, ncols - i * 512)
                    for kc in range(16):
                        S_.mm(PS(pb0 + i)[:, 0:nc_i], hT[:, kc * 128:(kc + 1) * 128],
                              slab.v(lambda a: a[:, kc, 0:nc_i]), start=(kc == 0), stop=(kc == 15))
                if upto == 'A2m' or (upto is not None and upto.startswith('A2g') and g not in upto[4:].split(',')):
                    continue
                if g == "qa":
                    sq = F4K.next()
                    S_.act(sq, pp, AF.Square)
                    ss = SM.next()
                    S_.red(ss, sq.h3(), ALU.add)
                    rs = rstd_from(ss, 8, 1.0 / HD)
                    qn = B2K.next()
                    S_.tt("dve", qn.h3(), pp.h3(), bch(rs, HD), ALU.mult)
                    def post(qn=qn, t=t):
                        pT = PSbf(4)
                        for h in range(8):
                            S_.tr(pT[:, h * 128:(h + 1) * 128], qn[:, h * 128:(h + 1) * 128], ident)
                        o = B2K.next()
                        S_.cp("act", o, pT)
                        S_.dma("pool", QT_D[t], o)
                    if deferred is not None:
                        deferred()
                    deferred = post
                elif g in ("ga", "gb"):
                    o = F4K.next()
                    S_.act(o, pp, AF.Silu)
                    S_.dma("pool", (GA_D if g == "ga" else GB_D)[t], o)
                elif g == "qi":
                    qn = B2K.next()
                    S_.cp("act", qn, pp)
                    def post(qn=qn, t=t):
                        pT = PSbf(5)
                        for h in range(8):
                            S_.tr(pT[:, h * 128:(h + 1) * 128], qn[:, h * 128:(h + 1) * 128], ident)
                        o = B2K.next()
                        S_.cp("dve", o, pT)
                        S_.dma("pool", QiT_D[t], o)
                    if deferred is not None:
                        deferred()
                    deferred = post
                elif g in ("qb", "kb"):
                    cs = CS.next()
                    S_.dma("sp", cs, cs_D[t])
                    cosb = bcm(cs[:, 0:64], NH)
                    sinb = bcm(cs[:, 64:128], NH)
                    xs_ = F4K.next()
                    S_.cp("act", xs_, pp)
                    x1 = xs_.h3()[:, :, 0:64]
                    x2 = xs_.h3()[:, :, 64:128]
                    ta = F4K.next()
                    tb = F4K.next()
                    S_.tt("dve", ta.h3()[:, :, 0:64], x1, cosb, ALU.mult)
                    S_.tt("dve", ta.h3()[:, :, 64:128], x2, sinb, ALU.mult)
                    S_.tt("dve", tb.h3()[:, :, 0:64], x1, sinb, ALU.mult)
                    S_.tt("dve", tb.h3()[:, :, 64:128], x2, cosb, ALU.mult)
                    rot = F4K.next()
                    S_.tt("dve", rot.h3()[:, :, 0:64], ta.h3()[:, :, 0:64], ta.h3()[:, :, 64:128], ALU.subtract)
                    S_.tt("dve", rot.h3()[:, :, 64:128], tb.h3()[:, :, 0:64], tb.h3()[:, :, 64:128], ALU.add)
                    o = B2K.next()
                    S_.tt("dve", o.h3(), rot.h3(), bch(decq if g == "qb" else deck, HD), ALU.mult)
                    S_.dma("pool", (QB_D if g == "qb" else KB_D)[t], o)
                elif g == "vb":
                    o = B2K.next()
                    S_.cp("act", o, pp)
                    S_.dma("pool", VB_D[t], o)
                else:
                    p0 = PS(pb0)
                    kn = B2K.next()
                    junk = F2K.next()
                    ss = SM.next()[:, 0:1]
                    S_.act(junk[:, 0:128], p0[:, 0:128], AF.Square, accum=ss)
                    rs = rstd_from(ss, 1, 1.0 / HD)
                    S_.ts("dve", kn[:, 0:128], p0[:, 0:128], rs, None, ALU.mult)
                    S_.cp("act", kn[:, 128:192], p0[:, 256:320])
                    S_.cp("act", kn[:, 192:256], p0[:, 256:320])
                    S_.cp("act", T(v_t[:, t, 0:128], vbufs[t]), p0[:, 128:256])
                    S_.cp("dve", T(wi_t[:, t, :], wibufs[t]), p0[:, 320:336])

                    def post(kn=kn, t=t):
                        pT = PSbf(4 + t % 2)
                        S_.tr(pT[:, 0:128], kn[:, 0:128], ident)
                        S_.tr(pT[:, 128:256], kn[:, 128:256], ident)
                        S_.ts("dve", T(kt_t[:, t * 128:(t + 1) * 128], kbufs[t]), pT[:, 0:128], gkq, None, ALU.mult)
                        S_.cp("act", T(kit_t[:, t * 128:(t + 1) * 128], kibufs[t]), pT[:, 128:256])
                    if deferred is not None:
                        deferred()
                    deferred = post
            if deferred is not None:
                deferred()
            c0 += ncols

        if upto is not None and upto.startswith('A2'):
            return finish()
        def indexer(qb):
            Iacc = IA[qb % 2]
            nk = qb + 1
            Sc = nk * 128
            qit = QP.next()
            S_.dma("sp", qit, QiT_D[qb])
            qt = QP.next()
            S_.dma("sp", qt, QT_D[qb])
            wq = T(wi_t[:, qb, :], wibufs[qb])
            dgw = DGW.next()
            S_.tt("dve", dgw.h3(NIH), bcm(ident, NIH), bch(wq, 128), ALU.mult)
            for gi, g0 in enumerate(range(0, nk, 4)):
                ge = min(g0 + 4, nk)
                ncl = (ge - g0) * 128
                Icol = Iacc[:, g0 * 128:g0 * 128 + ncl]
                accb = PS(4 + gi % 2)[:, 0:ncl]
                NP = NIH // 2
                for pp_ in range(NP + 1):
                    if pp_ < NP:
                        for hf in range(2):
                            lo_ = hf * 64
                            S_.mm(PS(2 * (pp_ % 2) + hf)[:, 0:ncl],
                                  qit.v(lambda a: a[lo_:lo_ + 64, pp_ * 128:(pp_ + 1) * 128]),
                                  T(kit_t[lo_:lo_ + 64, g0 * 128:g0 * 128 + ncl], kibufs[g0:ge]), start=True, stop=True)
                    if pp_ >= 1:
                        j = pp_ - 1
                        r2 = R16P.next().v(lambda a: a.rearrange("p (h n) -> p h n", h=2))
                        src = PS(2 * (j % 2), 2).v(lambda a: a.rearrange("p (h n) -> p h n", h=2))
                        S_.act(r2[:, :, 0:ncl], src[:, :, 0:ncl], AF.Relu)
                        for hf in range(2):
                            h = 2 * j + hf
                            S_.mm(accb, dgw[:, h * 128:(h + 1) * 128], r2[:, hf, 0:ncl],
                                  start=(h == 0), stop=(h == NIH - 1))
                S_.cp("act", Icol, accb)
            return qt

        def rest(qb, qt):
            Iacc = IA[qb % 2]
            nk = qb + 1
            Sc = nk * 128
            ga = F4K.next()
            S_.dma("sp", ga, GA_D[qb])
            Idiag = Iacc[:, qb * 128:(qb + 1) * 128]
            S_.tt("dve", Idiag, Idiag, negtri, ALU.add)
            lh = LH.next()
            hi = lh[:, 0:1]
            lo = lh[:, 1:2]
            S_.red(hi, Iacc[:, 0:Sc], ALU.max)
            tdb = B2K.next()
            td = T(tdb.ap.bitcast(F32)[:, 0:128], tdb.bufs)
            S_.stt(td, negtri, -2.0, Idiag, ALU.mult, ALU.add)
            S_.red(lo, td, ALU.min)
            if qb > 0:
                lo2 = lh[:, 2:3]
                S_.red(lo2, Iacc[:, 0:qb * 128], ALU.min)
                S_.tt("dve", lo, lo, lo2, ALU.min)
            if Sc > TOPK:
                w0 = lh[:, 3:4]
                S_.tt("dve", w0, hi, lo, ALU.subtract)
                stp = ST.next()
                S_.ts("dve", stp, pow2, w0, None, ALU.mult)
                mid = MID.next()
                S_.tt("dve", mid, lo, stp[:, 0:1], ALU.add)
                Sa = (nk // 2) * 128 if BIS_SPLIT else 0
                for it in range(NBIS):
                    cnt = SM.next()[:, 0:1]
                    sgn = SM.next()[:, 0:1]
                    gst = SM.next()[:, 0:1]
                    if Sa > 0:
                        S_.act(maskT[:, 0:Sa], Iacc[:, 0:Sa], AF.Sign, scale=-1.0, bias=mid, accum=sgn)
                    S_.ts("dve", maskb[:, Sa:Sc], Iacc[:, Sa:Sc], mid, 0.0, ALU.is_ge, ALU.add, accum=cnt)
                    if Sa > 0:
                        S_.stt(cnt, sgn, -0.5, cnt, ALU.mult, ALU.add)
                    S_.ts("dve", gst, cnt, TOPK - 0.5 * Sa - 0.25, stp[:, it:it + 1], ALU.is_ge, ALU.mult)
                    mid2 = MID.next()
                    S_.ts("dve", mid2, gst, mid, stp[:, it + 1:it + 2], ALU.add, ALU.subtract)
                    mid = mid2
                S_.tt("dve", lo, mid, stp[:, NBIS:NBIS + 1], ALU.subtract)
            S_.ts("dve", maskb[:, 0:Sc], Iacc[:, 0:Sc], lo, None, ALU.is_ge)
            if debug:
                S_.dma("pool", dtile(dbgI_d[qb][:, 0:Sc]), Iacc[:, 0:Sc])
                S_.dma("pool", dtile(dbgM_d[qb][:, 0:Sc]), maskb[:, 0:Sc])
                dt_ = SM.next()
                S_.cp("dve", dt_[:, 0:1], lo)
                S_.cp("dve", dt_[:, 1:2], hi)
                S_.dma("pool", dtile(dbgT_d[qb]), dt_)
            for k0 in range(0, nk, 8):
                ke = min(k0 + 8, nk)
                pT = PSbf(7)
                for kb in range(k0, ke):
                    S_.tr(pT[:, (kb - k0) * 128:(kb - k0 + 1) * 128], maskb[:, kb * 128:(kb + 1) * 128], ident)
                S_.cp("act", maskT[:, k0 * 128:ke * 128], pT[:, 0:(ke - k0) * 128])
            def qk(kb, ki):
                pb0 = 0 if ki % 2 == 0 else 2
                KTb = T(kt_t[:, kb * 128:(kb + 1) * 128], kbufs[kb])
                for hh in range(2):
                    S_.mm(PS(pb0 + hh), KTb, qt[:, hh * 512:(hh + 1) * 512], start=True, stop=True)

            korder = [kb_ for kb_ in (qb, qb - 1) if kb_ >= 0] + list(range(0, max(qb - 1, 0)))
            assert len(korder) == nk
            qk(korder[0], 0)
            for ki, kb in enumerate(korder):
                if ki + 1 < nk:
                    qk(korder[ki + 1], ki + 1)
                pp = PS(0, 2) if ki % 2 == 0 else PS(2, 2)
                e_ = B2K.next()
                S_.act(e_, pp, AF.Exp)
                p_ = B2K.next()
                mTb = maskT[:, kb * 128:(kb + 1) * 128]
                if kb >= qb - 1:
                    EB = EB0 if kb == qb else EB1
                    mn = B2K.next()
                    S_.tt("pool", mn.h3(), EB.h3(), bcm(mTb, NH), ALU.mult)
                    S_.tt("dve", p_, e_, mn, ALU.mult)
                else:
                    S_.tt("dve", p_.h3(), e_.h3(), bcm(mTb, NH), ALU.mult)
                Vb = T(v_t[:, kb, :], vbufs[kb])
                for h in range(NH):
                    S_.mm(PS(4 + h // 3)[:, (h % 3) * 129:(h % 3 + 1) * 129], p_[:, h * 128:(h + 1) * 128], Vb,
                          start=(ki == 0 and h % 3 == 0), stop=(ki == nk - 1), sgc=True)
            rden = SM.next()
            o_ = F4K.next()
            for bk in range(3):
                h0 = bk * 3
                nh = min(3, NH - h0)
                pv = PS(4 + bk).v(lambda a: a[:, 0:nh * 129].rearrange("p (h d) -> p h d", d=129))
                S_.recip(rden[:, h0:h0 + nh].v(lambda a: a.unsqueeze(2)), pv[:, :, 128:129])
                S_.tt("dve", o_[:, h0 * 128:(h0 + nh) * 128].h3(nh), pv[:, :, 0:128], bch(rden[:, h0:h0 + nh], HD), ALU.mult)
            y_ = B2K.next()
            S_.tt("dve", y_, o_, ga, ALU.mult)
            def postB(y_=y_, qb=qb):
                pT = PSbf(7)
                for h in range(NH):
                    S_.tr(pT[:, h * 128:(h + 1) * 128], y_[:, h * 128:(h + 1) * 128], ident)
                yo = B2K.next()
                S_.cp("act", yo, pT)
                S_.dma("pool", yTa_D[qb], yo)
            return postB

        qts = {0: indexer(0)}
        defB = None
        for qb in range(NTT):
            if qb + 1 < NTT:
                qts[qb + 1] = indexer(qb + 1)
            if defB is not None:
                defB()
            defB = rest(qb, qts.pop(qb))
        defB()

        if upto == 'B':
            return finish()
        wgenD = wgroups(wout_d, l, [(0, 1024), (1024, 1024)])
        slabsD0 = next(wgenD)
        S_.op("dve", lambda e: e.memset(R.ap, 0.0), [], [R])
        S_.op("dve", lambda e: e.memset(Rbf.ap, 0.0), [], [Rbf])
        defC = None
        for c in range(NTT):
            q_ = B2K.next()
            k_ = B2K.next()
            v_ = B2K.next()
            gb = F4K.next()
            S_.dma("sp", q_, QB_D[c])
            S_.dma("sp", k_, KB_D[c])
            S_.dma("sp", v_, VB_D[c])
            S_.dma("sp", gb, GB_D[c])
            pq, pk = PSbf(6), PSbf(7)
            for h in range(NH):
                S_.tr(pq[:, h * 128:(h + 1) * 128], q_[:, h * 128:(h + 1) * 128], ident)
            for h in range(NH):
                S_.tr(pk[:, h * 128:(h + 1) * 128], k_[:, h * 128:(h + 1) * 128], ident)
            qT = B2K.next()
            kT = B2K.next()
            S_.cp("act", qT, pq)
            S_.cp("dve", kT, pk)
            for h in range(NH):
                hs = slice(h * 128, (h + 1) * 128)
                S_.mm(PS(h // 4)[:, (h % 4) * 128:(h % 4 + 1) * 128], kT[:, hs], qT[:, hs],
                      start=(h % 4 == 0), stop=True, sgc=True)
            if defC is not None:
                defC()
                defC = None
            aT = B2K.next()
            S_.tt("dve", aT.h3(), PS(0, 2).h3(), bcm(causT, NH), ALU.mult)
            for h in range(NH):
                hs = slice(h * 128, (h + 1) * 128)
                po = PS(4 + h // 4)[:, (h % 4) * 128:(h % 4 + 1) * 128]
                S_.mm(po, aT[:, hs], v_[:, hs], start=(h % 4 == 0), stop=False, sgc=True)
                S_.mm(po, qT[:, hs], Rbf[:, hs], start=False, stop=True, sgc=True)
            for h in range(NH):
                hs = slice(h * 128, (h + 1) * 128)
                S_.mm(PS(2 + h // 4)[:, (h % 4) * 128:(h % 4 + 1) * 128], k_[:, hs], v_[:, hs],
                      start=(h % 4 == 0), stop=True, sgc=True)
            rt = F4K.next()
            S_.tt("dve", rt, PS(2, 2), R, ALU.add)
            S_.tt("dve", R, rt, GCt, ALU.mult)
            S_.cp("act", Rbf, R)
            po = PS(4, 2)
            s1 = SM.next()
            s2 = SM.next()
            S_.red(s1, po.h3(), ALU.add)
            sq = F4K.next()
            S_.act(sq, po, AF.Square)
            S_.red(s2, sq.h3(), ALU.add)
            mean = SM.next()
            S_.ts("dve", mean, s1, 1.0 / HD, None, ALU.mult)
            m2 = SM.next()
            S_.tt("dve", m2, mean, mean, ALU.mult)
            var = SM.next()
            S_.stt(var, s2, 1.0 / HD, m2, ALU.mult, ALU.subtract)
            rs = rstd_from(var, 8, 1.0)
            xc = F4K.next()
            S_.tt("dve", xc.h3(), po.h3(), bch(mean, HD), ALU.subtract)
            S_.tt("dve", xc.h3(), xc.h3(), bch(rs, HD), ALU.mult)
            S_.tt("dve", xc, xc, RG, ALU.mult)
            yb = B2K.next()
            S_.tt("dve", yb, xc, gb, ALU.mult)
            def postC(yb=yb, c=c):
                pT = PSbf(6)
                for h in range(NH):
                    S_.tr(pT[:, h * 128:(h + 1) * 128], yb[:, h * 128:(h + 1) * 128], ident)
                yo = B2K.next()
                S_.cp("act", yo, pT)
                S_.dma("pool", yTb_D[c], yo)
            defC = postC
        defC()

        if upto == 'C':
            return finish()
        g_bc = F8K[0]
        S_.dma("sp", g_bc, T(modraw_d[l, 2 * D:3 * D].partition_broadcast(128), [modraw_B]))
        for half_i in range(2):
            slabs = slabsD0 if half_i == 0 else next(wgenD)
            for t in range(NTT):
                yt = B4K.next()
                S_.dma("sp", T(yt.ap[:, 0:1024], yt.bufs), yTa_D[t])
                S_.dma("sp", T(yt.ap[:, 1024:2048], yt.bufs), yTb_D[t])
                pb0 = 0 if t % 2 == 0 else 2
                for i, slab in enumerate(slabs):
                    for kc in range(16):
                        S_.mm(PS(pb0 + i), yt[:, kc * 128:(kc + 1) * 128], slab.v(lambda a: a[:, kc, :]),
                              start=(kc == 0), stop=(kc == 15))
                cs_ = slice(half_i * 1024, (half_i + 1) * 1024)
                xh = F4K.next()
                src = xsrc[t] if half_i == 0 or True else None
                S_.dma("sp", xh, T(src.ap[:, cs_], src.bufs))
                tmp = F4K.next()
                S_.tt("dve", tmp, PS(pb0, 2), g_bc[:, cs_], ALU.mult)
                S_.tt("pool", tmp, tmp, xh, ALU.add)
                dst = xdst[t]
                S_.dma("pool", T(dst.ap[:, cs_], dst.bufs), tmp)

    return finish()


_CONST = None


def _consts():
    global _CONST
    if _CONST is not None:
        return _CONST
    i = np.arange(128)
    ident = np.eye(128, dtype=np.float32)
    causT = (i[:, None] <= i[None, :]).astype(np.float32)
    negtri = np.where(i[None, :] <= i[:, None], 0.0, NEG).astype(np.float32)
    invf = (10000.0 ** (-np.arange(64, dtype=np.float32) / np.float32(64))).astype(np.float32)
    h = np.arange(8, dtype=np.float64)
    log_g = np.log1p(-np.exp2(-5.0 - h))
    decq = np.exp(log_g[None, :] * (i[:, None] + 1.0)).astype(np.float32)
    deck = (np.exp(-log_g[None, :] * (i[:, None] + 1.0)) * (128.0 ** -0.5)).astype(np.float32)
    gc = np.repeat(np.exp(log_g * 128.0), 128).astype(np.float32)
    rel0 = i[None, :] - i[:, None]
    idx0 = _t5_bucket(rel0)
    idx1 = _t5_bucket(rel0 + 128)
    pow2 = np.ascontiguousarray(np.broadcast_to((2.0 ** -(np.arange(32) + 1.0))[None, :], (128, 32))).astype(np.float32)
    _CONST = dict(pow2=pow2, ident=ident, causT=causT, negtri=negtri, invf=invf, decq=decq, deck=deck, gc=gc,
                  idx0=idx0, idx1=idx1)
    return _CONST


def make_in_map(b, x, c, positions, rel_bias, norm_gain, w_mod, b_mod, w_in_p, q_norm_gain, k_norm_gain,
                ret_norm_gain, w_out, NL, NTT):
    C = _consts()
    S = NTT * 128
    bt0 = np.ascontiguousarray(rel_bias[C["idx0"]].transpose(0, 2, 1)).reshape(128, 1024)
    bt1 = np.ascontiguousarray(rel_bias[C["idx1"]].transpose(0, 2, 1)).reshape(128, 1024)
    b31 = np.ascontiguousarray(np.broadcast_to(rel_bias[31][None, :, None], (128, 8, 128))).reshape(128, 1024)
    return {
        "x": np.ascontiguousarray(x[b, :S]),
        "c": np.ascontiguousarray(c[b].reshape(16, 128).T),
        "positions": np.ascontiguousarray(positions[b, :S].reshape(NTT, 128).T),
        "norm_gain": norm_gain[:NL], "w_mod": w_mod[:NL], "b_mod": b_mod[:NL], "w_in": w_in_p[:NL],
        "q_norm_gain": np.ascontiguousarray(q_norm_gain[:NL, :, None]),
        "k_norm_gain": np.ascontiguousarray(k_norm_gain[:NL, :, None]),
        "ret_norm_gain": ret_norm_gain[:NL], "w_out": w_out[:NL],
        "ident": C["ident"], "causT": C["causT"], "negtri": C["negtri"], "invf": C["invf"],
        "decq": C["decq"], "deck": C["deck"], "gc": C["gc"], "pow2": C["pow2"],
        "bt0": bt0.astype(np.float32), "bt1": bt1.astype(np.float32), "b31": b31.astype(np.float32),
    }


def kernel(x, c, positions, rel_bias, norm_gain, w_mod, b_mod, w_in, q_norm_gain, k_norm_gain,
           ret_norm_gain, w_out):
    x = np.asarray(x, np.float32)
    B = x.shape[0]
    w_in_p = np.ascontiguousarray(np.asarray(w_in, np.float32)[:, :, PERM])
    nc = build()
    args = [np.asarray(a) for a in (c, positions, rel_bias, norm_gain, w_mod, b_mod)]
    rest = [np.asarray(a, np.float32) for a in (q_norm_gain, k_norm_gain, ret_norm_gain, w_out)]
    in_maps = [make_in_map(b, x, args[0].astype(np.float32), args[1].astype(np.int32), args[2].astype(np.float32),
                           args[3].astype(np.float32), args[4].astype(np.float32), args[5].astype(np.float32),
                           w_in_p, *rest, DEPTH, SEQ // 128) for b in range(B)]
    res = run_bass_kernel_spmd(nc, in_maps, core_ids=list(range(B)))
    return np.stack([np.asarray(r["out"], np.float32) for r in res.results], axis=0)
```
